# Optimizing a Trainium2 kernel written in Bass

```python
import jax, jax.numpy as jnp
from jax import lax
import numpy as np

D_MODEL = 2048
BATCH = 2
SEQ = 4096
DEPTH = 1
DEC_BATCH = 8
DEC_SEQ = 16
PAST_LEN = 2048

CHUNK = 64
QBLOCK = 128
ROPE_THETA = 500000.0
MLA_HEADS = 16
Q_LORA = 512
KV_LORA = 512
MLA_NOPE = 128
MLA_ROPE = 64
MLA_V = 128
DSA_HEADS = 16
DSA_KV_HEADS = 4
DSA_HEAD_DIM = 128
DSA_ROT = DSA_HEAD_DIM // 4
IDX_HEADS = 16
IDX_DIM = 128
IDX_ROT = IDX_DIM // 4
IDX_SCALE = (IDX_HEADS * IDX_DIM) ** -0.5
TOPK_MAX = 256
D_FF = 4 * D_MODEL
NORM_EPS = 1e-6

kernel_name = "hybrid_mla_dsa_streaming_encoder_step"


def _in_splits():
    return (Q_LORA, KV_LORA, MLA_ROPE,
            DSA_HEADS * DSA_HEAD_DIM, DSA_KV_HEADS * DSA_HEAD_DIM, DSA_KV_HEADS * DSA_HEAD_DIM,
            IDX_HEADS * IDX_DIM, IDX_DIM, IDX_HEADS,
            D_MODEL, D_MODEL)


def _rmsnorm(x, g):
    xf = x.astype(jnp.float32)
    y = xf * lax.rsqrt(jnp.mean(xf * xf, axis=-1, keepdims=True) + NORM_EPS) * g.astype(jnp.float32)
    return y.astype(x.dtype)


def _layernorm(x, g, b):
    xf = x.astype(jnp.float32)
    mu = jnp.mean(xf, axis=-1, keepdims=True)
    var = jnp.mean(jnp.square(xf - mu), axis=-1, keepdims=True)
    y = (xf - mu) * lax.rsqrt(var + NORM_EPS) * g.astype(jnp.float32) + b.astype(jnp.float32)
    return y.astype(x.dtype)


def _rope(x, pos, rot):
    inv = ROPE_THETA ** (-(jnp.arange(0, rot, 2, dtype=jnp.float32) / rot))
    ang = pos.astype(jnp.float32)[:, None] * inv[None, :]
    cos = jnp.cos(ang)[None, :, None, :]
    sin = jnp.sin(ang)[None, :, None, :]
    xr = x[..., :rot].astype(jnp.float32)
    x1, x2 = xr[..., : rot // 2], xr[..., rot // 2:]
    out = jnp.concatenate([x1 * cos - x2 * sin, x2 * cos + x1 * sin], axis=-1).astype(x.dtype)
    return jnp.concatenate([out, x[..., rot:]], axis=-1)


def _map_query_blocks(fn, q_pos, *qs):
    B, Lq = qs[0].shape[:2]
    blk = QBLOCK if Lq % QBLOCK == 0 else Lq
    nb = Lq // blk

    def split(a):
        return jnp.moveaxis(a.reshape((B, nb, blk) + a.shape[2:]), 1, 0)

    out = lax.map(lambda args: fn(*args), (q_pos.reshape(nb, blk),) + tuple(split(a) for a in qs))
    return jnp.moveaxis(out, 0, 1).reshape((B, Lq) + out.shape[3:])


def _mla_attention(q_nope, q_pe, k_nope, k_pe, v, q_pos, k_pos):
    B, Lq = q_nope.shape[:2]
    scale = (MLA_NOPE + MLA_ROPE) ** -0.5
    k_chunk = k_pos // CHUNK

    def block(qp, qn, qr):
        s = jnp.einsum('bqhd,bkhd->bhqk', qn, k_nope) + jnp.einsum('bqhr,bkr->bhqk', qr, k_pe)
        s = s.astype(jnp.float32) * scale
        vis = k_chunk[None, :] <= (qp // CHUNK)[:, None]
        p = jax.nn.softmax(jnp.where(vis[None, None], s, -jnp.inf), axis=-1).astype(v.dtype)
        return jnp.einsum('bhqk,bkhd->bqhd', p, v)

    o = _map_query_blocks(block, q_pos, q_nope, q_pe)
    return o.reshape(B, Lq, MLA_HEADS * MLA_V)


def _dsa_attention(q, q_idx, w_idx, k, v, k_idx, q_pos, k_pos, topk):
    B, Lq = q.shape[:2]
    G = DSA_KV_HEADS
    R = DSA_HEADS // DSA_KV_HEADS
    scale = DSA_HEAD_DIM ** -0.5
    k_chunk = k_pos // CHUNK
    gather = jax.vmap(lambda t, i: t[i])

    def block(qp, qb, qi, wi):
        blk = qp.shape[0]
        q_chunk = qp // CHUNK
        vis = k_chunk[None, :] <= q_chunk[:, None]
        rel = jax.nn.relu(jnp.einsum('bqhd,bkd->bqhk', qi, k_idx).astype(jnp.float32))
        score = jnp.einsum('bqh,bqhk->bqk', wi.astype(jnp.float32), rel)
        score = jnp.where(vis[None], score, -jnp.inf)
        _, sel = lax.top_k(score, topk)
        ok = k_chunk[sel] <= q_chunk[None, :, None]
        k_sel = gather(k, sel)
        v_sel = gather(v, sel)
        qg = qb.reshape(B, blk, G, R, DSA_HEAD_DIM)
        s = jnp.einsum('bqgrd,bqkgd->bqgrk', qg, k_sel).astype(jnp.float32) * scale
        s = jnp.where(ok[:, :, None, None, :], s, -jnp.inf)
        p = jax.nn.softmax(s, axis=-1).astype(v.dtype)
        o = jnp.einsum('bqgrk,bqkgd->bqgrd', p, v_sel)
        return o.reshape(B, blk, DSA_HEADS * DSA_HEAD_DIM)

    return _map_query_blocks(block, q_pos, q, q_idx, w_idx)


def _layer(x, pos, past, w_in, g_q_norm, g_kv_norm, w_uq, w_ukv, w_o_mla, w_o_dsa, w_out,
           ln1_g, ln1_b, w_up, w_down, ln2_g, ln2_b):
    B, L, _ = x.shape
    alpha = (2 * DEPTH) ** 0.25
    proj = jnp.einsum('bld,de->ble', x, w_in)
    cuts = np.cumsum(_in_splits())[:-1].tolist()
    (q_lat, kv_lat, k_pe, q_b, k_b, v_b, q_i, k_i, w_i, gate_a, gate_b) = jnp.split(proj, cuts, axis=-1)
    q = jnp.einsum('blr,rhe->blhe', _rmsnorm(q_lat, g_q_norm), w_uq)
    q_nope = q[..., :MLA_NOPE]
    q_pe = _rope(q[..., MLA_NOPE:], pos, MLA_ROPE)
    c_kv = _rmsnorm(kv_lat, g_kv_norm)
    k_pe = _rope(k_pe[:, :, None, :], pos, MLA_ROPE)[:, :, 0, :]
    q_b = _rope(q_b.reshape(B, L, DSA_HEADS, DSA_HEAD_DIM), pos, DSA_ROT)
    k_b = _rope(k_b.reshape(B, L, DSA_KV_HEADS, DSA_HEAD_DIM), pos, DSA_ROT)
    v_b = v_b.reshape(B, L, DSA_KV_HEADS, DSA_HEAD_DIM)
    q_i = _rope(q_i.reshape(B, L, IDX_HEADS, IDX_DIM), pos, IDX_ROT)
    k_i = _rope(k_i[:, :, None, :], pos, IDX_ROT)[:, :, 0, :]
    w_i = w_i * IDX_SCALE
    new_rows = (c_kv, k_pe, k_b, v_b, k_i)
    if past is None:
        c_kv_all, k_pe_all, k_b_all, v_b_all, k_i_all = new_rows
        k_pos = pos
    else:
        c_kv_all, k_pe_all, k_b_all, v_b_all, k_i_all = tuple(
            jnp.concatenate([p, n], axis=1) for p, n in zip(past, new_rows))
        k_pos = jnp.concatenate([jnp.arange(past[0].shape[1], dtype=jnp.int32), pos])
    Lk = k_pos.shape[0]
    kv = jnp.einsum('bkr,rhe->bkhe', c_kv_all, w_ukv)
    attn_a = _mla_attention(q_nope, q_pe, kv[..., :MLA_NOPE], k_pe_all, kv[..., MLA_NOPE:], pos, k_pos)
    topk = min(TOPK_MAX, Lk // 4)
    attn_b = _dsa_attention(q_b, q_i, w_i, k_b_all, v_b_all, k_i_all, pos, k_pos, topk)
    merged = (jax.nn.sigmoid(gate_a) * jnp.einsum('ble,ed->bld', attn_a, w_o_mla)
              + jax.nn.sigmoid(gate_b) * jnp.einsum('ble,ed->bld', attn_b, w_o_dsa))
    mix = jnp.einsum('bld,de->ble', merged, w_out)
    h = _layernorm(alpha * x + mix, ln1_g, ln1_b)
    f = jnp.einsum('blf,fd->bld', jnp.square(jax.nn.relu(jnp.einsum('bld,df->blf', h, w_up))), w_down)
    y = _layernorm(alpha * h + f, ln2_g, ln2_b)
    return y, new_rows


def setup_inputs(seed: int = 0) -> dict:
    key = jax.random.key(seed)
    ks = iter(jax.random.split(key, 48))
    f32 = jnp.float32

    def nrm(shape, scale):
        return jax.random.normal(next(ks), shape, f32) * scale

    beta = (8 * DEPTH) ** -0.25
    in_scales = (1.0, 1.0, 1.0, 1.0, 1.0, beta, 1.0, 1.0, 1.0, 1.0, 1.0)
    w_in = jnp.concatenate([nrm((DEPTH, D_MODEL, n), D_MODEL ** -0.5 * s)
                            for n, s in zip(_in_splits(), in_scales)], axis=-1)
    x_prompt = nrm((BATCH, SEQ, D_MODEL), 1.0)
    x_sample = nrm((DEC_BATCH, DEC_SEQ, D_MODEL), 1.0)
    cache_mla_latent = nrm((DEPTH, DEC_BATCH, PAST_LEN, KV_LORA), 1.0)
    cache_mla_rope = nrm((DEPTH, DEC_BATCH, PAST_LEN, MLA_ROPE), 1.0)
    cache_dsa_k = nrm((DEPTH, DEC_BATCH, PAST_LEN, DSA_KV_HEADS, DSA_HEAD_DIM), 1.0)
    cache_dsa_v = nrm((DEPTH, DEC_BATCH, PAST_LEN, DSA_KV_HEADS, DSA_HEAD_DIM), beta)
    cache_dsa_idx_k = nrm((DEPTH, DEC_BATCH, PAST_LEN, IDX_DIM), 1.0)
    g_q_norm = 1.0 + nrm((DEPTH, Q_LORA), 0.02)
    g_kv_norm = 1.0 + nrm((DEPTH, KV_LORA), 0.02)
    w_uq = nrm((DEPTH, Q_LORA, MLA_HEADS, MLA_NOPE + MLA_ROPE), Q_LORA ** -0.5)
    w_ukv = jnp.concatenate([nrm((DEPTH, KV_LORA, MLA_HEADS, MLA_NOPE), KV_LORA ** -0.5),
                             nrm((DEPTH, KV_LORA, MLA_HEADS, MLA_V), KV_LORA ** -0.5 * beta)], axis=-1)
    w_o_mla = nrm((DEPTH, MLA_HEADS * MLA_V, D_MODEL), (MLA_HEADS * MLA_V) ** -0.5 * beta)
    w_o_dsa = nrm((DEPTH, DSA_HEADS * DSA_HEAD_DIM, D_MODEL), (DSA_HEADS * DSA_HEAD_DIM) ** -0.5 * beta)
    w_out = nrm((DEPTH, D_MODEL, D_MODEL), D_MODEL ** -0.5 * beta)
    ln1_g = 1.0 + nrm((DEPTH, D_MODEL), 0.02)
    ln1_b = nrm((DEPTH, D_MODEL), 0.02)
    w_up = nrm((DEPTH, D_MODEL, D_FF), D_MODEL ** -0.5)
    w_down = nrm((DEPTH, D_FF, D_MODEL), D_FF ** -0.5 * beta)
    ln2_g = 1.0 + nrm((DEPTH, D_MODEL), 0.02)
    ln2_b = nrm((DEPTH, D_MODEL), 0.02)
    return {"x_prompt": x_prompt, "x_sample": x_sample,
            "cache_mla_latent": cache_mla_latent, "cache_mla_rope": cache_mla_rope,
            "cache_dsa_k": cache_dsa_k, "cache_dsa_v": cache_dsa_v, "cache_dsa_idx_k": cache_dsa_idx_k,
            "w_in": w_in, "g_q_norm": g_q_norm, "g_kv_norm": g_kv_norm, "w_uq": w_uq, "w_ukv": w_ukv,
            "w_o_mla": w_o_mla, "w_o_dsa": w_o_dsa, "w_out": w_out, "ln1_g": ln1_g, "ln1_b": ln1_b,
            "w_up": w_up, "w_down": w_down, "ln2_g": ln2_g, "ln2_b": ln2_b}


def reference(x_prompt, x_sample, cache_mla_latent, cache_mla_rope, cache_dsa_k, cache_dsa_v,
              cache_dsa_idx_k, w_in, g_q_norm, g_kv_norm, w_uq, w_ukv, w_o_mla, w_o_dsa, w_out,
              ln1_g, ln1_b, w_up, w_down, ln2_g, ln2_b):
    params = (w_in, g_q_norm, g_kv_norm, w_uq, w_ukv, w_o_mla, w_o_dsa, w_out,
              ln1_g, ln1_b, w_up, w_down, ln2_g, ln2_b)
    caches = (cache_mla_latent, cache_mla_rope, cache_dsa_k, cache_dsa_v, cache_dsa_idx_k)
    past_len = cache_mla_latent.shape[2]
    pos_p = jnp.arange(x_prompt.shape[1], dtype=jnp.int32)
    pos_s = past_len + jnp.arange(x_sample.shape[1], dtype=jnp.int32)
    hp, hs = x_prompt, x_sample
    rows_p, rows_s = [], []
    for layer in range(DEPTH):
        p_l = tuple(w[layer] for w in params)
        hp, new_p = _layer(hp, pos_p, None, *p_l)
        hs, new_s = _layer(hs, pos_s, tuple(c[layer] for c in caches), *p_l)
        rows_p.append(new_p)
        rows_s.append(new_s)
    p_mla_latent = jnp.stack([r[0] for r in rows_p])
    p_mla_rope = jnp.stack([r[1] for r in rows_p])
    p_dsa_k = jnp.stack([r[2] for r in rows_p])
    p_dsa_v = jnp.stack([r[3] for r in rows_p])
    p_dsa_idx_k = jnp.stack([r[4] for r in rows_p])
    s_mla_latent = jnp.stack([r[0] for r in rows_s])
    s_mla_rope = jnp.stack([r[1] for r in rows_s])
    s_dsa_k = jnp.stack([r[2] for r in rows_s])
    s_dsa_v = jnp.stack([r[3] for r in rows_s])
    s_dsa_idx_k = jnp.stack([r[4] for r in rows_s])
    return (hp, hs, p_mla_latent, p_mla_rope, p_dsa_k, p_dsa_v, p_dsa_idx_k,
            s_mla_latent, s_mla_rope, s_dsa_k, s_dsa_v, s_dsa_idx_k)
```

```python
import os
from contextlib import ExitStack
import numpy as np
import concourse.bass as bass
import concourse.mybir as mybir
from concourse.bass_utils import run_bass_kernel_spmd

F32 = mybir.dt.float32
BF16 = mybir.dt.bfloat16
AF = mybir.ActivationFunctionType
ALU = mybir.AluOpType
AX = mybir.AxisListType

ENGS = ("pe", "act", "dve", "pool", "sp")
EPOCH = 12000
DMA_RING = 8
DMA_EPOCH = 1500


class Inst:
    __slots__ = ("eng", "fn", "reads", "writes", "dma", "idx", "sig", "dsem")

    def __init__(self, eng, fn, reads, writes, dma):
        self.eng = eng
        self.fn = fn
        self.reads = reads
        self.writes = writes
        self.dma = dma
        self.sig = None
        self.dsem = None


class Prog:
    def __init__(self, nc):
        self.nc = nc
        self.insts = []
        self.barriers = []
        self.stack = ExitStack()

    def op(self, eng, fn, reads=(), writes=()):
        i = Inst(eng, fn, tuple(reads), tuple(writes), False)
        i.idx = len(self.insts)
        self.insts.append(i)
        return i

    def barrier(self):
        self.barriers.append(len(self.insts))

    def dma(self, q, out, in_, reads=(), writes=()):
        def fn(e, out=out, in_=in_):
            return e.dma_start(out=out, in_=in_)
        i = Inst(q, fn, tuple(reads), tuple(writes), True)
        i.idx = len(self.insts)
        self.insts.append(i)
        return i

    def finish(self):
        nc = self.nc
        insts = self.insts
        last_w = {}
        readers = {}
        need = [None] * len(insts)
        for i in insts:
            d = {}
            for r in i.reads:
                w = last_w.get(r)
                if w is not None:
                    d[w] = True
            for r in i.writes:
                w = last_w.get(r)
                if w is not None:
                    d.setdefault(w, False)
                for rd in readers.get(r, ()):
                    if rd != i.idx:
                        d.setdefault(rd, False)
            for r in i.reads:
                readers.setdefault(r, []).append(i.idx)
            for r in i.writes:
                last_w[r] = i.idx
                readers[r] = []
            keep = []
            best = {}
            for didx, raw in d.items():
                dd = insts[didx]
                if dd.dma:
                    keep.append(didx)
                    continue
                if (not i.dma) and dd.eng == i.eng and (i.eng == "pe" or (not raw and i.eng != "pool")):
                    continue
                if didx > best.get(dd.eng, -1):
                    best[dd.eng] = didx
            need[i.idx] = keep + list(best.values())
        for bpos in self.barriers:
            lastc = {}
            lastd = {e: [] for e in ENGS}
            for i in insts[:bpos]:
                if i.dma:
                    lastd[i.eng].append(i.idx)
                else:
                    lastc[i.eng] = i.idx
            bdeps = list(lastc.values())
            for e in ENGS:
                bdeps += lastd[e][-DMA_RING:]
            seen = set()
            for i in insts[bpos:]:
                if i.eng in seen:
                    continue
                seen.add(i.eng)
                cur = set(need[i.idx])
                for d_ in bdeps:
                    if d_ not in cur and not (insts[d_].eng == i.eng and not insts[d_].dma and not i.dma):
                        need[i.idx].append(d_)
                if len(seen) == len(ENGS):
                    break
        for i in insts:
            for didx in need[i.idx]:
                insts[didx].sig = True
        cnt = {e: 0 for e in ENGS}
        dcnt = {e: 0 for e in ENGS}
        for i in insts:
            if i.dma:
                i.dsem = dcnt[i.eng]
                dcnt[i.eng] += 1
            elif i.sig:
                i.sig = cnt[i.eng]
                cnt[i.eng] += 1
            else:
                i.sig = None
        sems = {}
        for e in ENGS:
            n = max(1, (cnt[e] + EPOCH - 1) // EPOCH)
            sems[e] = [self.stack.enter_context(nc.semaphore(f"s_{e}_{k}")) for k in range(n)]
        dsems = {}
        for e in ENGS:
            if dcnt[e]:
                nep = (dcnt[e] + DMA_RING * DMA_EPOCH - 1) // (DMA_RING * DMA_EPOCH)
                dsems[e] = [[self.stack.enter_context(nc.semaphore(f"d_{e}_{k}_{r}")) for r in range(DMA_RING)]
                            for k in range(nep)]

        def dma_sem_val(q, n):
            r = n % DMA_RING
            m = n // DMA_RING
            return dsems[q][m // DMA_EPOCH][r], 16 * ((m % DMA_EPOCH) + 1)

        def sig_sem_val(e, k):
            return sems[e][k // EPOCH], (k % EPOCH) + 1

        per_eng = {e: [i for i in insts if i.eng == e] for e in ENGS}
        self.stats = {e: len(per_eng[e]) for e in ENGS}
        self.stats.update({f"sig_{e}": cnt[e] for e in ENGS})
        self.stats.update({f"dma_{e}": dcnt[e] for e in ENGS})

        def emit(e, eng):
            waited = {}

            def wait(sem, val):
                key = id(sem)
                if waited.get(key, 0) >= val:
                    return
                waited[key] = val
                eng.wait_ge(sem, val)

            for i in per_eng[e]:
                for didx in sorted(need[i.idx]):
                    d = insts[didx]
                    if d.dma:
                        s, v = dma_sem_val(d.eng, d.dsem)
                    else:
                        s, v = sig_sem_val(d.eng, d.sig)
                    wait(s, v)
                if i.dma:
                    if i.dsem >= DMA_RING:
                        s, v = dma_sem_val(e, i.dsem - DMA_RING)
                        wait(s, v)
                    s, v = dma_sem_val(e, i.dsem)
                    i.fn(eng).then_inc(s, 16)
                else:
                    ins = i.fn(eng)
                    if i.sig is not None:
                        s, v = sig_sem_val(e, i.sig)
                        ins.then_inc(s, 1)
            n = dcnt[e]
            for k in range(max(0, n - DMA_RING), n):
                s, v = dma_sem_val(e, k)
                wait(s, v)

        with nc.Block() as block:
            @block.tensor
            def _(eng):
                emit("pe", eng)

            @block.scalar
            def _(eng):
                emit("act", eng)

            @block.vector
            def _(eng):
                emit("dve", eng)

            @block.gpsimd
            def _(eng):
                emit("pool", eng)

            @block.sync
            def _(eng):
                emit("sp", eng)
        self.stack.close()


D = 2048
KC = 16
SEQ = 4096
PAST = 2048
DEC = 16
NOWN = 1040
LKS = PAST + DEC
C_QLAT, C_KVLAT, C_KPE, C_QB, C_KB, C_VB, C_QI, C_KI, C_WI, C_GA, C_GB = (
    0, 512, 1024, 1088, 3136, 3648, 4160, 6208, 6336, 6352, 8400)
EPS = 1e-6
ALPHA = 2.0 ** 0.25
IDX_SCALE = (16 * 128) ** -0.5
MLA_SCALE = 192.0 ** -0.5
DSA_SCALE = 128.0 ** -0.5
TOPK = 256
NBIS = 24
STAGE = int(os.environ.get("K_STAGE", "9"))
DEBUG = bool(int(os.environ.get("K_DEBUG", "0")))

OWN_TILES = [(128 * i, 128) for i in range(8)] + [(1024, 16)]
KT_P = [(128 * i, 128) for i in range(32)]
KT_S = [(128 * i, 128) for i in range(16)] + [(2048, 16)]
SLOTS = [
    dict(name="A", tok0=0, n=512, kv=0, kts=KT_P[:16], masked=lambda kt: True, tiles=[0, 1, 2, 3]),
    dict(name="B", tok0=512, n=512, kv=0, kts=KT_P, masked=lambda kt: kt >= 16, tiles=[4, 5, 6, 7]),
    dict(name="S", tok0=1024, n=16, kv=1, kts=KT_S, masked=lambda kt: False, tiles=[8]),
]


def build():
    nc = bass.Bass("TRN2", target_bir_lowering=False)
    P = Prog(nc)

    def din(name, shape, dt=F32):
        return nc.dram_tensor(name, list(shape), dt, kind="ExternalInput").ap()

    def dout(name, shape):
        return nc.dram_tensor(name, list(shape), F32, kind="ExternalOutput").ap()

    def dscr(name, shape, dt=BF16):
        return nc.dram_tensor(name, list(shape), dt, kind=("ExternalOutput" if DEBUG else "Internal")).ap()

    xk = din("xk", [SEQ, D])
    xo = din("xo", [NOWN, D])
    c_lat = din("c_lat", [PAST, 512])
    c_rope = din("c_rope", [PAST, 64])
    c_k = din("c_k", [PAST, 512])
    c_v = din("c_v", [PAST, 512])
    c_idx = din("c_idx", [PAST, 128])
    w_in = din("w_in", [D, 10448])
    w_uq = din("w_uq", [512, 3072])
    w_ukv = din("w_ukv", [512, 16, 256])
    w_o_mla = din("w_o_mla", [D, D])
    w_o_dsa = din("w_o_dsa", [D, D])
    w_out = din("w_out", [D, D])
    w_up = din("w_up", [D, 8192])
    w_down = din("w_down", [8192, D])
    gq_d = din("gq_bc", [128, 512])
    gkv_d = din("gkv_bc", [128, 512])
    ln_d = din("ln_bc", [4, 128, D])
    tabK = din("tabK", [33, 128, 192])
    tabQ = din("tabQ", [9, 128, 256])
    qc_d = din("qc", [2, 128, 512])
    kcc_d = din("kcc", [128, 32])
    tcc_d = din("tcc", [128, 8])
    scb_d = din("scb", [128, SEQ])
    ident_d = din("ident", [128, 128])

    y = dout("y", [NOWN, D])
    pk = [dout("pk_lat", [SEQ, 512]), dout("pk_rope", [SEQ, 64]), dout("pk_idx", [SEQ, 128]),
          dout("pk_k", [SEQ, 512]), dout("pk_v", [SEQ, 512])]
    sk = [dout("sk_lat", [DEC, 512]), dout("sk_rope", [DEC, 64]), dout("sk_idx", [DEC, 128]),
          dout("sk_k", [DEC, 512]), dout("sk_v", [DEC, 512])]

    LKP = [SEQ, LKS]
    s_ckvT = [dscr(f"s_ckvT{k}", [128, 4, LKP[k]]) for k in range(2)]
    s_kbT = [dscr(f"s_kbT{k}", [128, 4, LKP[k]]) for k in range(2)]
    s_kiT = [dscr(f"s_kiT{k}", [128, LKP[k]]) for k in range(2)]
    s_kpeT = [dscr(f"s_kpeT{k}", [64, LKP[k]]) for k in range(2)]
    s_vb = [dscr(f"s_vb{k}", [128, (LKP[k] + 127) // 128, 512]) for k in range(2)]
    s_qnT = dscr("s_qnT", [128, 16, NOWN])
    s_qpeT = dscr("s_qpeT", [64, 16, NOWN])
    s_qbT = dscr("s_qbT", [128, 16, NOWN])
    s_qiT = dscr("s_qiT", [128, 16, NOWN])
    s_attn = [dscr("s_attnA", [128, 16, NOWN]), dscr("s_attnB", [128, 16, NOWN])]
    s_h = dscr("s_h", [NOWN, D], F32)

    cstack = ExitStack()

    uniq = [0]

    def sb(stack, name, shape, dt):
        uniq[0] += 1
        return stack.enter_context(nc.sbuf_tensor(f"{name}_{uniq[0]}", list(shape), dt))

    def ps(stack, name, shape, dt=F32):
        return stack.enter_context(nc.psum_tensor(name, list(shape), dt))

    identf = sb(cstack, "identf", [128, 128], F32)
    identb = sb(cstack, "identb", [128, 128], BF16)
    st = sb(cstack, "st", [128, 16], F32)
    P.dma("sp", identf[:], ident_d, writes=["identf"])
    P.op("dve", lambda e: e.tensor_copy(out=identb[:], in_=identf[:]), reads=["identf"], writes=["identb"])

    pb = [ps(cstack, f"pb{i}", [128, 512]) for i in range(8)]
    PB = [f"pb{i}" for i in range(8)]

    tog = [0]

    def evac_eng():
        tog[0] ^= 1
        return "act" if tog[0] else "dve"

    def copy(eng, out, in_, reads, writes, scale=None):
        if eng == "act":
            if scale is None:
                P.op("act", lambda e: e.activation(out=out, in_=in_, func=AF.Copy), reads, writes)
            else:
                P.op("act", lambda e: e.activation(out=out, in_=in_, func=AF.Copy, scale=scale), reads, writes)
        else:
            if scale is None:
                P.op(eng, lambda e: e.tensor_copy(out=out, in_=in_), reads, writes)
            else:
                P.op(eng, lambda e: e.tensor_scalar(out=out, in0=in_, scalar1=scale, scalar2=None, op0=ALU.mult), reads, writes)

    def mm(out, lhsT, rhs, start, stop, reads, writes):
        P.op("pe", lambda e: e.matmul(out, lhsT=lhsT, rhs=rhs, start=start, stop=stop), reads, writes)

    def tt(eng, out, a, b, op, reads, writes):
        P.op(eng, lambda e: e.tensor_tensor(out=out, in0=a, in1=b, op=op), reads, writes)

    def rope(eng, xv, half, cosv, sinv, tmps, nt, H, xres, tres):
        tA, tB, tC, tD = [t[:nt, 0:H, 0:half] for t in tmps]
        x1 = xv[:, :, 0:half]
        x2 = xv[:, :, half:2 * half]
        tt(eng, tA, x1, cosv, ALU.mult, [xres, tres], ["rtA"])
        tt(eng, tB, x2, sinv, ALU.mult, [xres, tres], ["rtB"])
        tt(eng, tC, x2, cosv, ALU.mult, [xres, tres], ["rtC"])
        tt(eng, tD, x1, sinv, ALU.mult, [xres, tres], ["rtD"])
        tt(eng, x1, tA, tB, ALU.subtract, ["rtA", "rtB"], [xres])
        tt(eng, x2, tC, tD, ALU.add, ["rtC", "rtD"], [xres])

    def rms_scale(psrc, psname, nt, width, junk):
        P.op("pool", lambda e: e.memset(st[:, 0:1], 0.0), [], ["st0"])
        P.op("act", lambda e: e.activation(out=junk[:nt, 0:width], in_=psrc, func=AF.Square, accum_out=st[:nt, 0:1]),
             [psname, "st0"], ["junk", "st0"])
        P.op("act", lambda e: e.activation(out=st[:nt, 1:2], in_=st[:nt, 0:1], func=AF.Sqrt, scale=1.0 / width, bias=eps_t[:nt, 0:1]),
             ["st0", "eps"], ["st1"])
        P.op("dve", lambda e: e.reciprocal(out=st[:nt, 2:3], in_=st[:nt, 1:2]), ["st1"], ["st2"])

    eps_t = sb(cstack, "eps_t", [128, 1], F32)
    P.op("pool", lambda e: e.memset(eps_t[:], EPS), [], ["eps"])

    wq = [0]

    def load_x_T(stack_bufs, src_rows, nt, xT_dst3, xT_res, b, bankfn=None, preloaded=False):
        xs = stack_bufs[b]
        if not preloaded:
            P.dma("act", xs[:nt, :], src_rows, writes=[f"xs{b}"])
        for g in range(4):
            bank = bankfn(g) if bankfn else g % 2
            for c in range(4):
                kc = g * 4 + c
                mm(pb[bank][:, c * 128:c * 128 + nt], xs[:nt, kc * 128:(kc + 1) * 128], identf[:nt, :nt], True, True,
                   [f"xs{b}", "identf"], [PB[bank]])
            copy(evac_eng(), xT_dst3(g), pb[bank][:].rearrange("p (a b) -> p a b", a=4)[:, :, 0:nt], [PB[bank]], [xT_res])

    with ExitStack() as s1:
        wk = sb(s1, "wk", [128, KC, 1728], BF16)
        xs = [sb(s1, f"xs{i}", [128, D], F32) for i in range(3)]
        xT = [sb(s1, f"xT{i}", [128, KC, 128], BF16) for i in range(3)]
        rows = [sb(s1, f"rows{i}", [128, 1728], F32) for i in range(3)]
        rowsb = [sb(s1, f"rowsb{i}", [128, 1728], BF16) for i in range(3)]
        kT = [sb(s1, f"kT{i}", [128, 10, 512], BF16) for i in range(2)]
        kgrp = {}
        tab = [sb(s1, f"tab{i}", [128, 192], F32) for i in range(3)]
        tmps = [sb(s1, f"rt{i}", [128, 16, 32], F32) for i in range(4)]
        gkv = sb(s1, "gkv", [128, 512], F32)
        junk = sb(s1, "junk", [128, 512], F32)
        P.dma("sp", gkv[:], gkv_d, writes=["gkv"])
        for (c0, c1, d0) in [(C_KVLAT, C_KVLAT + 512, 0), (C_KPE, C_KPE + 64, 512), (C_KI, C_KI + 128, 576),
                             (C_KB, C_KB + 512, 704), (C_VB, C_VB + 512, 1216)]:
            P.dma("pool", wk[:, :, d0:d0 + (c1 - c0)], w_in[:, c0:c1].rearrange("(kc p) n -> p kc n", p=128), writes=["wk"])

        def kside_finish(kv, t, nt, b):
            rb = rowsb[b]
            RB = f"rowsb{b}"
            P.dma("sp", s_vb[kv][:, t, :], rb[:, 1216:1728], reads=[RB], writes=[f"s_vb{kv}"])
            g, q = t // 4, t % 4
            if (kv, g) not in kgrp:
                kgrp[(kv, g)] = len(kgrp) % 2
            gi = kgrp[(kv, g)]
            k_ = kT[gi]
            KT_ = f"kT{gi}"
            c0 = q * 128
            for c in range(4):
                mm(pb[6][:, c * 128:c * 128 + nt], rb[:nt, c * 128:(c + 1) * 128], identb[:nt, :nt], True, True, [RB, "identb"], [PB[6]])
            copy(evac_eng(), k_[:, 0:4, c0:c0 + nt], pb[6][:].rearrange("p (a b) -> p a b", a=4)[:, :, 0:nt], [PB[6]], [KT_])
            for c in range(4):
                mm(pb[7][:, c * 128:c * 128 + nt], rb[:nt, 704 + c * 128:704 + (c + 1) * 128], identb[:nt, :nt], True, True, [RB, "identb"], [PB[7]])
            copy(evac_eng(), k_[:, 4:8, c0:c0 + nt], pb[7][:].rearrange("p (a b) -> p a b", a=4)[:, :, 0:nt], [PB[7]], [KT_])
            mm(pb[6][:, 0:nt], rb[:nt, 576:704], identb[:nt, :nt], True, True, [RB, "identb"], [PB[6]])
            mm(pb[6][0:64, 128:128 + nt], rb[:nt, 512:576], identb[:nt, :nt], True, True, [RB, "identb"], [PB[6]])
            copy(evac_eng(), k_[:, 8, c0:c0 + nt], pb[6][:, 0:nt], [PB[6]], [KT_])
            copy(evac_eng(), k_[0:64, 9, c0:c0 + nt], pb[6][0:64, 128:128 + nt], [PB[6]], [KT_])
            if q == 3 or nt < 128:
                tk = g * 512
                w = c0 + nt
                P.dma("sp", s_ckvT[kv][:, :, tk:tk + w], k_[:, 0:4, 0:w], reads=[KT_], writes=[f"s_ckvT{kv}"])
                P.dma("sp", s_kbT[kv][:, :, tk:tk + w], k_[:, 4:8, 0:w], reads=[KT_], writes=[f"s_kbT{kv}"])
                P.dma("sp", s_kiT[kv][:, tk:tk + w], k_[:, 8, 0:w], reads=[KT_], writes=[f"s_kiT{kv}"])
                P.dma("sp", s_kpeT[kv][0:64, tk:tk + w], k_[0:64, 9, 0:w], reads=[KT_], writes=[f"s_kpeT{kv}"])

        def kside_proj(kv, t, nt, src_rows, tab_idx, outs, orow0, b):
            load_x_T(xs, src_rows, nt, lambda g: xT[b][:, 4 * g:4 * g + 4, 0:nt], f"xT{b}", b, preloaded=True)
            XT = f"xT{b}"
            for kc in range(KC):
                for (bank, c0, c1) in [(2, 0, 512), (3, 512, 704), (4, 704, 1216), (5, 1216, 1728)]:
                    mm(pb[bank][:nt, 0:c1 - c0], xT[b][:, kc, 0:nt], wk[:, kc, c0:c1], kc == 0, kc == KC - 1, [XT, "wk"], [PB[bank]])
            r = rows[b]
            R = f"rows{b}"
            rms_scale(pb[2][:nt, :], PB[2], nt, 512, junk)
            P.op("dve", lambda e: e.scalar_tensor_tensor(out=r[:nt, 0:512], in0=pb[2][:nt, :], scalar=st[:nt, 2:3], in1=gkv[:nt, :],
                                                        op0=ALU.mult, op1=ALU.mult), [PB[2], "st2", "gkv"], [R])
            copy("act", r[:nt, 512:704], pb[3][:nt, 0:192], [PB[3]], [R])
            copy("act", r[:nt, 704:1216], pb[4][:nt, :], [PB[4]], [R])
            copy("dve", r[:nt, 1216:1728], pb[5][:nt, :], [PB[5]], [R])
            tb = tab[b]
            TB = f"tab{b}"
            rope("pool", r[:nt, 512:576].rearrange("p (h d) -> p h d", h=1), 32,
                 tb[:nt, 0:32].rearrange("p (h d) -> p h d", h=1), tb[:nt, 32:64].rearrange("p (h d) -> p h d", h=1), tmps, nt, 1, R, TB)
            rope("pool", r[:nt, 576:704].rearrange("p (h d) -> p h d", h=1), 16,
                 tb[:nt, 64:80].rearrange("p (h d) -> p h d", h=1), tb[:nt, 128:144].rearrange("p (h d) -> p h d", h=1), tmps, nt, 1, R, TB)
            rope("pool", r[:nt, 704:1216].rearrange("p (h d) -> p h d", h=4), 16,
                 tb[:nt, 64:128].rearrange("p (h d) -> p h d", h=4), tb[:nt, 128:192].rearrange("p (h d) -> p h d", h=4), tmps, nt, 4, R, TB)
            for oi, (c0, c1) in enumerate([(0, 512), (512, 576), (576, 704), (704, 1216), (1216, 1728)]):
                P.dma("sp", outs[oi][orow0:orow0 + nt, :], r[:nt, c0:c1], reads=[R])
            copy("act", rowsb[b][:nt, :], r[:nt, :], [R], [f"rowsb{b}"])
            if pending_fin:
                kside_finish(*pending_fin.pop())
            pending_fin.append((kv, t, nt, b))

        n = 0
        pending_fin = []
        ptiles = [(0, t, 128, xk[t * 128:(t + 1) * 128, :], t, pk, t * 128) for t in range(32)]
        ptiles.append((1, 16, DEC, xo[1024:1040, :], 32, sk, 0))

        def issue_loads(i):
            kv_, t_, nt_, src_, tab_idx_, _, _ = ptiles[i]
            b_ = i % 3
            P.dma("act", xs[b_][:nt_, :], src_, writes=[f"xs{b_}"])
            P.dma("act", tab[b_][:], tabK[tab_idx_], writes=[f"tab{b_}"])
        issue_loads(0)
        issue_loads(1)
        for i, pt in enumerate(ptiles):
            if i + 2 < len(ptiles):
                issue_loads(i + 2)
            kside_proj(*pt, n % 3)
            n += 1
        kside_finish(*pending_fin.pop())
        n0 = n

        def issue_cache(t):
            b_ = (n0 + t) % 3
            R_ = f"rows{b_}"
            r0 = t * 128
            P.dma("act", rows[b_][:, 0:512], c_lat[r0:r0 + 128, :], writes=[R_])
            P.dma("act", rows[b_][:, 512:576], c_rope[r0:r0 + 128, :], writes=[R_])
            P.dma("act", rows[b_][:, 576:704], c_idx[r0:r0 + 128, :], writes=[R_])
            P.dma("act", rows[b_][:, 704:1216], c_k[r0:r0 + 128, :], writes=[R_])
            P.dma("act", rows[b_][:, 1216:1728], c_v[r0:r0 + 128, :], writes=[R_])
        issue_cache(0)
        issue_cache(1)
        for t in range(16):
            if t + 2 < 16:
                issue_cache(t + 2)
            b = (n0 + t) % 3
            copy("act" if t % 2 else "dve", rowsb[b][:, :], rows[b][:, :], [f"rows{b}"], [f"rowsb{b}"])
            kside_finish(1, t, 128, b)
            n += 1

    held = set()
    rr = [0]

    def getb():
        for _ in range(32):
            b = rr[0] % 8
            rr[0] += 1
            if b not in held:
                return b
        raise RuntimeError("no psum bank")

    def holdb():
        b = getb()
        held.add(b)
        return b

    def relb(b):
        held.discard(b)

    widx = sb(cstack, "widx", [128, 9, 16], F32)
    s_xTo = dscr("s_xTo", [128, KC, NOWN])
    s_hT = dscr("s_hT", [128, KC, NOWN])

    def wview(w_ap, r0, r1, c0, c1):
        return w_ap[r0:r1, c0:c1].rearrange("(kc p) n -> p kc n", p=128)

    if STAGE >= 2:
        P.barrier()
        with ExitStack() as s2:
            xTo = sb(s2, "xTo", [128, KC, NOWN], BF16)
            xs = [sb(s2, f"xs{i}", [128, D], F32) for i in range(2)]
            wbuf = [sb(s2, f"w{i}", [128, KC, 512], BF16) for i in range(3)]
            wuq = sb(s2, "wuq", [128, 4, 3072], BF16)
            qlT = sb(s2, "qlT", [128, 4, NOWN], BF16)
            qln = [sb(s2, f"qln{i}", [128, 512], BF16) for i in range(2)]
            qrow = [sb(s2, f"qrow{i}", [128, 512], F32) for i in range(3)]
            qrb = [sb(s2, f"qrb{i}", [128, 512], BF16) for i in range(3)]
            qT4 = [sb(s2, f"qT4{i}", [128, 4, 512], BF16) for i in range(3)]
            qgrp = {}

            def q4_for(key):
                if key not in qgrp:
                    qgrp[key] = len(qgrp) % 3
                return qgrp[key]
            SLOT_OF = {ti_: (0 if ti_ < 4 else (1 if ti_ < 8 else 2)) for ti_ in range(9)}
            SLOT_T0 = [0, 512, 1024]
            tq = sb(s2, "tq", [128, 9, 256], F32)
            tmps = [sb(s2, f"rt{i}", [128, 16, 32], F32) for i in range(4)]
            gq = sb(s2, "gq", [128, 512], F32)
            junk = sb(s2, "junk", [128, 512], F32)
            P.dma("sp", gq[:], gq_d, writes=["gq"])
            P.dma("sp", tq[:], tabQ.rearrange("t p c -> p t c"), writes=["tq"])
            P.dma("pool", wuq[:], w_uq.rearrange("(kc p) n -> p kc n", p=128), writes=["wuq"])
            for ti, (t0, nt) in enumerate(OWN_TILES):
                load_x_T(xs, xo[t0:t0 + nt, :], nt, lambda g, t0=t0, nt=nt: xTo[:, 4 * g:4 * g + 4, t0:t0 + nt], "xTo", ti % 2,
                         bankfn=lambda g: getb())
            P.dma("sp", s_xTo, xTo[:], reads=["xTo"], writes=["s_xTo"])
            wloads = [(C_QLAT, 512)] + [(C_QB + 512 * k_, 512) for k_ in range(4)] + [(C_QI + 512 * k_, 512) for k_ in range(4)] + [(C_WI, 16)]
            issued = {}

            def get_w(k):
                for kk in (k, k + 1):
                    if kk < len(wloads) and kk not in issued:
                        buf, res = wbuf[kk % 3], f"w{kk % 3}"
                        c0_, nc_ = wloads[kk]
                        P.dma("pool", buf[:, :, 0:nc_], wview(w_in, 0, D, c0_, c0_ + nc_), writes=[res])
                        issued[kk] = (buf, res)
                return issued[k]

            wb, WB = get_w(0)
            for ti, (t0, nt) in enumerate(OWN_TILES):
                bank = getb()
                for kc in range(KC):
                    mm(pb[bank][:nt, :], xTo[:, kc, t0:t0 + nt], wb[:, kc, 0:512], kc == 0, kc == KC - 1, ["xTo", WB], [PB[bank]])
                rms_scale(pb[bank][:nt, :], PB[bank], nt, 512, junk)
                q_ = qln[ti % 2]
                QN = f"qln{ti % 2}"
                P.op("dve", lambda e, q_=q_, bank=bank, nt=nt: e.scalar_tensor_tensor(
                    out=q_[:nt, :], in0=pb[bank][:nt, :], scalar=st[:nt, 2:3], in1=gq[:nt, :], op0=ALU.mult, op1=ALU.mult),
                    [PB[bank], "st2", "gq"], [QN])
                tb = getb()
                for c in range(4):
                    mm(pb[tb][:, c * 128:c * 128 + nt], q_[:nt, c * 128:(c + 1) * 128], identb[:nt, :nt], True, True, [QN, "identb"], [PB[tb]])
                copy(evac_eng(), qlT[:, 0:4, t0:t0 + nt], pb[tb][:].rearrange("p (a b) -> p a b", a=4)[:, :, 0:nt], [PB[tb]], ["qlT"])
            n = 0
            pend2 = []
            for blk in range(8):
                for ti, (t0, nt) in enumerate(OWN_TILES):
                    i2 = n % 3
                    n += 1
                    bank = getb()
                    for rc in range(4):
                        mm(pb[bank][:nt, 0:384], qlT[:, rc, t0:t0 + nt], wuq[:, rc, blk * 384:(blk + 1) * 384], rc == 0, rc == 3,
                           ["qlT", "wuq"], [PB[bank]])
                    qr_, QR = qrow[i2], f"qrow{i2}"
                    copy("act", qr_[:nt, 0:384], pb[bank][:nt, 0:384], [PB[bank]], [QR])
                    rope("pool", qr_[:nt, 0:384].rearrange("p (h d) -> p h d", h=2)[:, :, 128:192], 32,
                         tq[:nt, ti, 0:64].rearrange("p (h d) -> p h d", h=2), tq[:nt, ti, 64:128].rearrange("p (h d) -> p h d", h=2),
                         tmps, nt, 2, QR, "tq")
                    qb_, QB = qrb[i2], f"qrb{i2}"
                    copy("dve", qb_[:nt, 0:384], qr_[:nt, 0:384], [QR], [QB])

                    def fin_b(qb_=qb_, QB=QB, nt=nt, t0=t0, blk=blk, ti=ti):
                        tb = getb()
                        for i in range(2):
                            mm(pb[tb][:, i * 128:i * 128 + nt], qb_[:nt, i * 192:i * 192 + 128], identb[:nt, :nt], True, True, [QB, "identb"], [PB[tb]])
                        for i in range(2):
                            mm(pb[tb][0:64, (2 + i) * 128:(2 + i) * 128 + nt], qb_[:nt, i * 192 + 128:(i + 1) * 192], identb[:nt, :nt], True, True,
                               [QB, "identb"], [PB[tb]])
                        sl = SLOT_OF[ti]
                        gq = q4_for(("b", blk, sl))
                        q4, Q4 = qT4[gq], f"qT4{gq}"
                        c0 = t0 - SLOT_T0[sl]
                        pv = pb[tb][:].rearrange("p (a b) -> p a b", a=4)
                        copy(evac_eng(), q4[:, 0:2, c0:c0 + nt], pv[:, 0:2, 0:nt], [PB[tb]], [Q4])
                        copy(evac_eng(), q4[0:64, 2:4, c0:c0 + nt], pv[0:64, 2:4, 0:nt], [PB[tb]], [Q4])
                        if ti in (3, 7, 8):
                            w = c0 + nt
                            s0_ = SLOT_T0[sl]
                            P.dma("sp", s_qnT[:, 2 * blk:2 * blk + 2, s0_:s0_ + w], q4[:, 0:2, 0:w], reads=[Q4], writes=["s_qnT"])
                            P.dma("sp", s_qpeT[0:64, 2 * blk:2 * blk + 2, s0_:s0_ + w], q4[0:64, 2:4, 0:w], reads=[Q4], writes=["s_qpeT"])
                    if pend2:
                        pend2.pop()()
                    pend2.append(fin_b)
            for (kbase, dst, DST) in [(1, s_qbT, "s_qbT"), (5, s_qiT, "s_qiT")]:
                for blk in range(4):
                    wb, WB = get_w(kbase + blk)
                    for ti, (t0, nt) in enumerate(OWN_TILES):
                        i2 = n % 3
                        n += 1
                        bank = getb()
                        for kc in range(KC):
                            mm(pb[bank][:nt, :], xTo[:, kc, t0:t0 + nt], wb[:, kc, 0:512], kc == 0, kc == KC - 1, ["xTo", WB], [PB[bank]])
                        qr_, QR = qrow[i2], f"qrow{i2}"
                        copy("act", qr_[:nt, :], pb[bank][:nt, :], [PB[bank]], [QR])
                        rope("pool", qr_[:nt, :].rearrange("p (h d) -> p h d", h=4), 16,
                             tq[:nt, ti, 128:192].rearrange("p (h d) -> p h d", h=4), tq[:nt, ti, 192:256].rearrange("p (h d) -> p h d", h=4),
                             tmps, nt, 4, QR, "tq")
                        qb_, QB = qrb[i2], f"qrb{i2}"
                        copy("dve", qb_[:nt, :], qr_[:nt, :], [QR], [QB])

                        def fin_c(qb_=qb_, QB=QB, nt=nt, t0=t0, blk=blk, dst=dst, DST=DST, ti=ti):
                            tb = getb()
                            for i in range(4):
                                mm(pb[tb][:, i * 128:i * 128 + nt], qb_[:nt, i * 128:(i + 1) * 128], identb[:nt, :nt], True, True, [QB, "identb"], [PB[tb]])
                            sl = SLOT_OF[ti]
                            gq = q4_for((DST, blk, sl))
                            q4, Q4 = qT4[gq], f"qT4{gq}"
                            c0 = t0 - SLOT_T0[sl]
                            copy(evac_eng(), q4[:, 0:4, c0:c0 + nt], pb[tb][:].rearrange("p (a b) -> p a b", a=4)[:, :, 0:nt], [PB[tb]], [Q4])
                            if ti in (3, 7, 8):
                                w = c0 + nt
                                s0_ = SLOT_T0[sl]
                                P.dma("sp", dst[:, 4 * blk:4 * blk + 4, s0_:s0_ + w], q4[:, 0:4, 0:w], reads=[Q4], writes=[DST])
                        if pend2:
                            pend2.pop()()
                        pend2.append(fin_c)
            if pend2:
                pend2.pop()()
            wb, WB = get_w(9)
            for ti, (t0, nt) in enumerate(OWN_TILES):
                bank = getb()
                for kc in range(KC):
                    mm(pb[bank][:nt, 0:16], xTo[:, kc, t0:t0 + nt], wb[:, kc, 0:16], kc == 0, kc == KC - 1, ["xTo", WB], [PB[bank]])
                copy("act", widx[:nt, ti, :], pb[bank][:nt, 0:16], [PB[bank]], ["widx"], scale=IDX_SCALE)

    def attend(slot, h, smm, scale, maskop, vrhs, vres, out_scr, OUT, bufs, qres):
        n = slot["n"]
        tok0 = slot["tok0"]
        kts = slot["kts"]
        nq = (n + 127) // 128
        accb = [holdb() for _ in range(nq)]
        LA = 2
        pend = []
        nk = len(kts)
        for idx in range(nk + LA):
            if idx < nk:
                kt, (k0, kr) = idx, kts[idx]
                bank = getb()
                smm(bank, kt, k0, kr, n)
                i3 = bufs["n"][0] % 4
                bufs["n"][0] += 1
                pT, PT = bufs["pT"][i3], f"pT{i3}"
                P.op("act", lambda e, pT=pT, bank=bank, kr=kr: e.activation(out=pT[:kr, 0:n], in_=pb[bank][:kr, 0:n], func=AF.Exp, scale=scale),
                     [PB[bank]], [PT])
                maskop(kt, kr, n, pT, PT)
                pend.append((pT, PT))
            j = idx - LA
            if j >= 0:
                kt, (k0, kr) = j, kts[j]
                pT, PT = pend[j]
                for qs in range(nq):
                    q0 = qs * 128
                    qr = min(128, n - q0)
                    mm(pb[accb[qs]][:qr, 0:129], pT[:kr, q0:q0 + qr], vrhs(kt, kr), kt == 0, kt == nk - 1, [PT, vres], [PB[accb[qs]]])
        tb = getb()
        for qs in range(nq):
            q0 = qs * 128
            qr = min(128, n - q0)
            a = accb[qs]
            i2 = bufs["n"][0] % 2
            bufs["n"][0] += 1
            rc, RC = bufs["rc"][i2], f"rc{i2}"
            ob, OB = bufs["ob"][i2], f"ob{i2}"
            P.op("dve", lambda e, rc=rc, a=a, qr=qr: e.reciprocal(out=rc[:qr, 0:1], in_=pb[a][:qr, 128:129]), [PB[a]], [RC])
            P.op("dve", lambda e, ob=ob, rc=rc, a=a, qr=qr: e.tensor_scalar(out=ob[:qr, :], in0=pb[a][:qr, 0:128], scalar1=rc[:qr, 0:1],
                                                                           scalar2=None, op0=ALU.mult), [PB[a], RC], [OB])
            mm(pb[tb][:, q0:q0 + qr], ob[:qr, :], identb[:qr, :qr], True, True, [OB, "identb"], [PB[tb]])
        for a in accb:
            relb(a)
        i2 = bufs["n"][0] % 2
        bufs["n"][0] += 1
        oT, OT = bufs["oT"][i2], f"oT{i2}"
        copy(evac_eng(), oT[:, 0:n], pb[tb][:, 0:n], [PB[tb]], [OT])
        P.dma("sp", out_scr[:, h, tok0:tok0 + n], oT[:, 0:n], reads=[OT], writes=[OUT])

    def attn_bufs(stk):
        return dict(n=[0], pT=[sb(stk, f"pT{i}", [128, 512], BF16) for i in range(4)],
                    rc=[sb(stk, f"rc{i}", [128, 1], F32) for i in range(2)],
                    ob=[sb(stk, f"ob{i}", [128, 128], BF16) for i in range(2)],
                    oT=[sb(stk, f"oT{i}", [128, 512], BF16) for i in range(2)])

    if STAGE >= 3:
        P.barrier()
        with ExitStack() as s3:
            ckvT = sb(s3, "ckvT", [128, 4, SEQ], BF16)
            kpeT = sb(s3, "kpeT", [128, SEQ], BF16)
            wkv = [sb(s3, f"wkv{i}", [128, 4, 256], BF16) for i in range(2)]
            knT = [sb(s3, f"knT{i}", [128, SEQ], BF16) for i in range(2)]
            Vh = [sb(s3, f"Vh{i}", [128, 32, 129], BF16) for i in range(2)]
            qn = [sb(s3, f"qn{i}", [128, 512], BF16) for i in range(2)]
            qpe = [sb(s3, f"qpe{i}", [64, 512], BF16) for i in range(2)]
            qc = [sb(s3, f"qc{i}", [128, 512], F32) for i in range(2)]
            kcc = sb(s3, "kcc", [128, 32], F32)
            bufs = attn_bufs(s3)
            for i in range(2):
                P.op("pool", lambda e, i=i: e.memset(Vh[i][:, :, 128:129], 1.0), [], [f"Vh{i}"])
                P.dma("sp", qc[i][:], qc_d[i], writes=[f"qc{i}"])
            P.dma("sp", kcc[:], kcc_d, writes=["kcc"])
            wukv_v = w_ukv.rearrange("(kc p) h e -> p kc h e", p=128)
            hn = 0
            qunits = [(kv_, h_, si_) for kv_ in range(2) for h_ in range(16) for si_, sl_ in enumerate(SLOTS) if sl_["kv"] == kv_]
            q3done = set()

            def issue_q3(u):
                if u >= len(qunits) or u in q3done:
                    return
                q3done.add(u)
                _, h_, si_ = qunits[u]
                n_, tok0_ = SLOTS[si_]["n"], SLOTS[si_]["tok0"]
                j_ = u % 2
                P.dma("sp", qn[j_][:, 0:n_], s_qnT[:, h_, tok0_:tok0_ + n_], reads=["s_qnT"], writes=[f"qn{j_}"])
                P.dma("sp", qpe[j_][:, 0:n_], s_qpeT[:, h_, tok0_:tok0_ + n_], reads=["s_qpeT"], writes=[f"qpe{j_}"])
            for kv in range(2):
                Lk = LKP[kv]
                kts_all = KT_P if kv == 0 else KT_S
                P.dma("sp", ckvT[:, :, 0:Lk], s_ckvT[kv], reads=[f"s_ckvT{kv}"], writes=["ckvT"])
                P.dma("sp", kpeT[0:64, 0:Lk], s_kpeT[kv], reads=[f"s_kpeT{kv}"], writes=["kpeT"])
                for h in range(16):
                    i2 = hn % 2
                    hn += 1
                    wk_, WK = wkv[i2], f"wkv{i2}"
                    kn_, KN = knT[i2], f"knT{i2}"
                    v_, VH = Vh[i2], f"Vh{i2}"
                    P.dma("pool", wk_[:], wukv_v[:, :, h, :], writes=[WK])
                    for s0 in range(0, Lk, 512):
                        sn = min(512, Lk - s0)
                        bank = getb()
                        for rc in range(4):
                            mm(pb[bank][:, 0:sn], wk_[:, rc, 0:128], ckvT[:, rc, s0:s0 + sn], rc == 0, rc == 3, [WK, "ckvT"], [PB[bank]])
                        copy(evac_eng(), kn_[:, s0:s0 + sn], pb[bank][:, 0:sn], [PB[bank]], [KN])
                    full = [(i, k0) for i, (k0, kr) in enumerate(kts_all) if kr == 128]
                    for g0 in range(0, len(full), 4):
                        grp = full[g0:g0 + 4]
                        bank = getb()
                        for gi, (i, k0) in enumerate(grp):
                            for rc in range(4):
                                mm(pb[bank][:, gi * 128:(gi + 1) * 128], ckvT[:, rc, k0:k0 + 128], wk_[:, rc, 128:256], rc == 0, rc == 3,
                                   [WK, "ckvT"], [PB[bank]])
                        ng = len(grp)
                        copy(evac_eng(), v_[:, grp[0][0]:grp[0][0] + ng, 0:128], pb[bank][:].rearrange("p (a b) -> p a b", a=4)[:, 0:ng, :],
                             [PB[bank]], [VH])
                    for i, (k0, kr) in enumerate(kts_all):
                        if kr < 128:
                            bank = getb()
                            for rc in range(4):
                                mm(pb[bank][:kr, 0:128], ckvT[:, rc, k0:k0 + kr], wk_[:, rc, 128:256], rc == 0, rc == 3, [WK, "ckvT"], [PB[bank]])
                            copy(evac_eng(), v_[:kr, i, 0:128], pb[bank][:kr, 0:128], [PB[bank]], [VH])
                    for si, slot in enumerate(SLOTS):
                        if slot["kv"] != kv:
                            continue
                        n, tok0 = slot["n"], slot["tok0"]
                        u = qunits.index((kv, h, si))
                        issue_q3(u)
                        issue_q3(u + 1)
                        j2 = u % 2
                        q_, QN = qn[j2], f"qn{j2}"
                        p_, QP = qpe[j2], f"qpe{j2}"

                        def smm(bank, kt, k0, kr, n, q_=q_, p_=p_, QN=QN, QP=QP, kn_=kn_, KN=KN):
                            mm(pb[bank][:kr, 0:n], kn_[:, k0:k0 + kr], q_[:, 0:n], True, False, [KN, QN], [PB[bank]])
                            mm(pb[bank][:kr, 0:n], kpeT[0:64, k0:k0 + kr], p_[0:64, 0:n], False, True, ["kpeT", QP], [PB[bank]])

                        def maskop(kt, kr, n, pT, PT, slot=slot, si=si):
                            if slot["masked"](kt):
                                P.op("dve", lambda e: e.scalar_tensor_tensor(out=pT[:kr, 0:n], in0=qc[si][:kr, 0:n], scalar=kcc[:kr, kt:kt + 1],
                                                                             in1=pT[:kr, 0:n], op0=ALU.is_ge, op1=ALU.mult),
                                     [f"qc{si}", "kcc", PT], [PT])

                        attend(slot, h, smm, MLA_SCALE, maskop, lambda kt, kr, v_=v_: v_[:kr, kt, 0:129], VH, s_attn[0], "s_attnA", bufs, QN)
    if STAGE >= 4:
        P.barrier()
        with ExitStack() as s4:
            kiT = sb(s4, "kiT", [128, SEQ], BF16)
            scb = sb(s4, "scb", [128, SEQ], BF16)
            score = [sb(s4, f"score{i}", [128, SEQ], F32) for i in range(2)]
            jmask = [sb(s4, f"jmask{i}", [128, SEQ], BF16) for i in range(2)]
            maskT = sb(s4, "maskT", [128, 32, 512], BF16)
            diag = [sb(s4, f"diag{i}", [128, 16, 128], BF16) for i in range(2)]
            qi = [sb(s4, f"qi{i}", [128, 16, 128], BF16) for i in range(2)]
            relu = [sb(s4, f"relu{i}", [128, 512], BF16) for i in range(4)]
            pen = [sb(s4, f"pen{i}", [128, 512], F32) for i in range(2)]
            mnb = [sb(s4, f"mnb{i}", [128, 8], F32) for i in range(2)]
            bs = [sb(s4, f"bs{i}", [128, 8], F32) for i in range(2)]
            tcc = sb(s4, "tcc", [128, 8], F32)
            pw2 = sb(s4, "pw2", [128, NBIS + 1], F32)
            steps = [sb(s4, f"steps{i}", [128, NBIS + 1], F32) for i in range(2)]
            for k_ in range(NBIS + 1):
                P.op("pool", lambda e, k_=k_: e.memset(pw2[:, k_:k_ + 1], 2.0 ** -(k_ + 1)), [], ["pw2"])
            kbT = sb(s4, "kbT", [128, 4, SEQ], BF16)
            vbs = sb(s4, "vbs", [128, 32, 4, 129], BF16)
            qb = [sb(s4, f"qb{i}", [128, 512], BF16) for i in range(2)]
            bufs = attn_bufs(s4)
            P.dma("pool", scb[:], scb_d, writes=["scb"])
            P.dma("sp", tcc[:], tcc_d, writes=["tcc"])
            P.op("pool", lambda e: e.memset(vbs[:, :, :, 128:129], 1.0), [], ["vbs"])
            rnbox = [0]
            qbn = 0
            for si, slot in enumerate(SLOTS):
                kv = slot["kv"]
                kts = slot["kts"]
                Lk = kts[-1][0] + kts[-1][1]
                n, tok0 = slot["n"], slot["tok0"]
                P.dma("sp", kiT[:, 0:Lk], s_kiT[kv][:, 0:Lk], reads=[f"s_kiT{kv}"], writes=["kiT"])
                blocks = [(s0, min(512, Lk - s0)) for s0 in range(0, Lk, 512)]
                def gen_idx(ti, par, Lk=Lk, blocks=blocks, slot=slot, kts=kts, tok0=tok0):
                    nonlocal_rn = rnbox
                    t0, tr = OWN_TILES[ti]
                    q_, QI = qi[par], f"qi{par}"
                    dg, DG = diag[par], f"diag{par}"
                    sc, SC = score[par], f"score{par}"
                    mn_, MN = mnb[par], f"mnb{par}"
                    P.dma("sp", q_[:, :, 0:tr], s_qiT[:, :, t0:t0 + tr], reads=["s_qiT"], writes=[QI])
                    for h in range(16):
                        P.op("pool", lambda e, h=h: e.tensor_scalar(out=dg[:tr, h, 0:tr], in0=identb[:tr, :tr],
                                                                    scalar1=widx[:tr, ti, h:h + 1], scalar2=None, op0=ALU.mult),
                             ["identb", "widx"], [DG])
                    yield
                    for bi, (s0, sn) in enumerate(blocks):
                        sbank = holdb()
                        LA = 2
                        pend = []
                        for hh in range(16 + LA):
                            if hh < 16:
                                h = hh
                                bank = getb()
                                mm(pb[bank][:tr, 0:sn], q_[:, h, 0:tr], kiT[:, s0:s0 + sn], True, True, [QI, "kiT"], [PB[bank]])
                                r_, RL = relu[nonlocal_rn[0] % 4], f"relu{nonlocal_rn[0] % 4}"
                                nonlocal_rn[0] += 1
                                P.op("act", lambda e, r_=r_, bank=bank, sn=sn: e.activation(out=r_[:tr, 0:sn], in_=pb[bank][:tr, 0:sn], func=AF.Relu),
                                     [PB[bank]], [RL])
                                pend.append((r_, RL))
                            h = hh - LA
                            if h >= 0:
                                r_, RL = pend[h]
                                mm(pb[sbank][:tr, 0:sn], dg[:tr, h, 0:tr], r_[:tr, 0:sn], h == 0, h == 15, [DG, RL], [PB[sbank]])
                            if hh % 4 == 3:
                                yield
                        P.op("dve", lambda e, sbank=sbank, bi=bi, sn=sn: e.tensor_reduce(out=mn_[:tr, bi:bi + 1], in_=pb[sbank][:tr, 0:sn], axis=AX.X, op=ALU.min),
                             [PB[sbank]], [MN])
                        if slot["name"] != "S":
                            p_, PN = pen[bi % 2], f"pen{bi % 2}"
                            P.op("pool", lambda e, p_=p_, s0=s0, sn=sn: e.tensor_scalar(out=p_[:tr, 0:sn], in0=scb[:tr, s0:s0 + sn], scalar1=tcc[:tr, ti:ti + 1],
                                                                                      scalar2=1e30, op0=ALU.is_gt, op1=ALU.mult),
                                 ["scb", "tcc"], [PN])
                            tt("dve", sc[:tr, s0:s0 + sn], pb[sbank][:tr, 0:sn], p_[:tr, 0:sn], ALU.subtract, [PB[sbank], PN], [SC])
                        else:
                            copy("dve", sc[:tr, s0:s0 + sn], pb[sbank][:tr, 0:sn], [PB[sbank]], [SC])
                        relb(sbank)
                        yield

                def gen_bis(ti, par, Lk=Lk, blocks=blocks, slot=slot, kts=kts, tok0=tok0):
                    t0, tr = OWN_TILES[ti]
                    sc, SC = score[par], f"score{par}"
                    mn_, MN = mnb[par], f"mnb{par}"
                    jm, JM = jmask[par], f"jmask{par}"
                    b_ = bs[par]
                    B = lambda nme: f"b_{nme}{par}"
                    nb = len(blocks)
                    st_, STP = steps[par], f"steps{par}"
                    P.op("dve", lambda e: e.tensor_reduce(out=b_[:tr, 5:6], in_=sc[:tr, 0:Lk], axis=AX.X, op=ALU.max), [SC], [B("mx")])
                    P.op("dve", lambda e: e.tensor_reduce(out=b_[:tr, 0:1], in_=mn_[:tr, 0:nb], axis=AX.X, op=ALU.min), [MN], [B("lo")])
                    tt("dve", b_[:tr, 1:2], b_[:tr, 5:6], b_[:tr, 0:1], ALU.subtract, [B("mx"), B("lo")], [B("rng")])
                    P.op("dve", lambda e: e.tensor_scalar(out=st_[:tr, :], in0=pw2[:tr, :], scalar1=b_[:tr, 1:2], scalar2=None, op0=ALU.mult),
                         ["pw2", B("rng")], [STP])
                    tt("dve", b_[:tr, 2:3], b_[:tr, 0:1], st_[:tr, 0:1], ALU.add, [B("lo"), STP], [B("cand")])
                    yield
                    for it in range(NBIS):
                        P.op("dve", lambda e: e.tensor_scalar(out=jm[:tr, 0:Lk], in0=sc[:tr, 0:Lk], scalar1=b_[:tr, 2:3], scalar2=0.0,
                                                            op0=ALU.is_ge, op1=ALU.add, accum_out=b_[:tr, 3:4]),
                             [SC, B("cand")], [JM, B("cnt")])
                        P.op("dve", lambda e: e.tensor_scalar(out=b_[:tr, 4:5], in0=b_[:tr, 3:4], scalar1=TOPK - 0.5, scalar2=0.5,
                                                            op0=ALU.is_ge, op1=ALU.subtract), [B("cnt")], [B("inc")])
                        P.op("dve", lambda e, it=it: e.scalar_tensor_tensor(out=b_[:tr, 2:3], in0=b_[:tr, 4:5], scalar=st_[:tr, it:it + 1], in1=b_[:tr, 2:3],
                                                                           op0=ALU.mult, op1=ALU.add), [B("inc"), STP, B("cand")], [B("cand")])
                        yield
                    tt("dve", b_[:tr, 0:1], b_[:tr, 2:3], st_[:tr, NBIS:NBIS + 1], ALU.subtract, [B("cand"), STP], [B("lo")])
                    P.op("dve", lambda e: e.tensor_scalar(out=jm[:tr, 0:Lk], in0=sc[:tr, 0:Lk], scalar1=b_[:tr, 0:1], scalar2=None, op0=ALU.is_ge),
                         [SC, B("lo")], [JM])
                    tq0 = t0 - tok0
                    full = [(i, k0) for i, (k0, kr) in enumerate(kts) if kr == 128]
                    for g0 in range(0, len(full), 4):
                        grp = full[g0:g0 + 4]
                        bank = getb()
                        for gi, (i, k0) in enumerate(grp):
                            mm(pb[bank][:, gi * 128:gi * 128 + tr], jm[:tr, k0:k0 + 128], identb[:tr, :tr], True, True, [JM, "identb"], [PB[bank]])
                        ng = len(grp)
                        copy("act", maskT[:, grp[0][0]:grp[0][0] + ng, tq0:tq0 + tr],
                             pb[bank][:].rearrange("p (a b) -> p a b", a=4)[:, 0:ng, 0:tr], [PB[bank]], ["maskT"])
                        yield
                    for i, (k0, kr) in enumerate(kts):
                        if kr < 128:
                            bank = getb()
                            mm(pb[bank][:kr, 0:tr], jm[:tr, k0:k0 + kr], identb[:tr, :tr], True, True, [JM, "identb"], [PB[bank]])
                            copy("act", maskT[:kr, i, tq0:tq0 + tr], pb[bank][:kr, 0:tr], [PB[bank]], ["maskT"])

                tl = slot["tiles"]
                for _ in gen_idx(tl[0], 0):
                    pass
                for k in range(len(tl)):
                    gb = gen_bis(tl[k], k % 2)
                    gi_ = gen_idx(tl[k + 1], (k + 1) % 2) if k + 1 < len(tl) else iter(())
                    bdone = idone = False
                    while not (bdone and idone):
                        if not bdone:
                            try:
                                next(gb)
                            except StopIteration:
                                bdone = True
                        for _ in range(2):
                            if not idone:
                                try:
                                    next(gi_)
                                except StopIteration:
                                    idone = True
                nkt = len(kts)
                P.dma("sp", kbT[:, :, 0:Lk], s_kbT[kv][:, :, 0:Lk], reads=[f"s_kbT{kv}"], writes=["kbT"])
                P.dma("sp", vbs[:, 0:nkt, :, 0:128], s_vb[kv][:, 0:nkt, :].rearrange("p t (g d) -> p t g d", g=4), reads=[f"s_vb{kv}"], writes=["vbs"])
                mtog = [0]
                q5done = set()

                def issue_q5(h_, n=n, tok0=tok0, base=qbn):
                    if h_ >= 16 or h_ in q5done:
                        return
                    q5done.add(h_)
                    j_ = (base + h_) % 2
                    P.dma("sp", qb[j_][:, 0:n], s_qbT[:, h_, tok0:tok0 + n], reads=["s_qbT"], writes=[f"qb{j_}"])
                for h in range(16):
                    g = h // 4
                    issue_q5(h)
                    issue_q5(h + 1)
                    q_, QB = qb[qbn % 2], f"qb{qbn % 2}"
                    qbn += 1

                    def smm(bank, kt, k0, kr, n, q_=q_, QB=QB, g=g):
                        mm(pb[bank][:kr, 0:n], kbT[:, g, k0:k0 + kr], q_[:, 0:n], True, True, ["kbT", QB], [PB[bank]])

                    def maskop(kt, kr, n, pT, PT):
                        mtog[0] ^= 1
                        tt("dve" if mtog[0] else "pool", pT[:kr, 0:n], pT[:kr, 0:n], maskT[:kr, kt, 0:n], ALU.mult, [PT, "maskT"], [PT])

                    attend(slot, h, smm, DSA_SCALE, maskop, lambda kt, kr, g=g: vbs[:kr, kt, g, 0:129], "vbs", s_attn[1], "s_attnB", bufs, QB)

    if STAGE >= 5:
        TG = [(0, 512), (512, 512), (1024, 16)]
        P.barrier()
        with ExitStack() as s6:
            mrg = sb(s6, "mrg", [128, KC, NOWN], BF16)
            lnc = [sb(s6, f"lnc{i}", [128, D], F32) for i in range(2)]
            junk2 = sb(s6, "junk2", [128, D], F32)

            def layernorm(xv, XR, nt, gi, lnc=lnc, junk2=junk2):
                P.op("dve", lambda e: e.tensor_reduce(out=st[:nt, 4:5], in_=xv, axis=AX.X, op=ALU.add), [XR], ["st4"])
                P.op("dve", lambda e: e.tensor_scalar(out=st[:nt, 5:6], in0=st[:nt, 4:5], scalar1=1.0 / D, scalar2=None, op0=ALU.mult), ["st4"], ["st5"])
                P.op("dve", lambda e: e.tensor_scalar(out=xv, in0=xv, scalar1=st[:nt, 5:6], scalar2=None, op0=ALU.subtract), [XR, "st5"], [XR])
                if DEBUG and gi == 100:
                    P.dma("sp", s_dd[0], xv, reads=[XR], writes=["s_dd"])
                rms_scale(xv, XR, nt, D, junk2)
                P.op("dve", lambda e: e.scalar_tensor_tensor(out=xv, in0=xv, scalar=st[:nt, 2:3], in1=lnc[0][:nt, :], op0=ALU.mult, op1=ALU.mult),
                     [XR, "st2", "lnc0"], [XR])
                if DEBUG and gi == 100:
                    P.dma("sp", s_dd[1], xv, reads=[XR], writes=["s_dd"])
                    P.dma("sp", s_dd[2], lnc[0][:], reads=["lnc0"], writes=["s_dd"])
                    P.dma("sp", s_dd[3], lnc[1][:], reads=["lnc1"], writes=["s_dd"])
                tt("pool", xv, xv, lnc[1][:nt, :], ALU.add, [XR, "lnc1"], [XR])

            with ExitStack() as s6a:
                xTo = sb(s6a, "xTo", [128, KC, NOWN], BF16)
                att = sb(s6a, "att", [128, KC, NOWN], BF16)
                wbuf = [sb(s6a, f"w{i}", [128, KC, 512], BF16) for i in range(4)]
                sg = [sb(s6a, f"sg{i}", [128, 512], F32) for i in range(2)]
                tmpm = [sb(s6a, f"tmpm{i}", [128, 512], F32) for i in range(2)]
                P.dma("sp", xTo[:], s_xTo, reads=["s_xTo"], writes=["xTo"])
                sn_ = 0
                issued6 = {}

                def get_w6(k):
                    for kk in (k, k + 1):
                        if kk < 8 and kk not in issued6:
                            br_, cb_ = kk // 4, kk % 4
                            cg_ = C_GA if br_ == 0 else C_GB
                            wo_ap_ = w_o_mla if br_ == 0 else w_o_dsa
                            i0 = (2 * kk) % 4
                            P.dma("pool", wbuf[i0][:], wview(w_in, 0, D, cg_ + 512 * cb_, cg_ + 512 * cb_ + 512), writes=[f"w{i0}"])
                            P.dma("pool", wbuf[i0 + 1][:], wview(wo_ap_, 0, D, 512 * cb_, 512 * cb_ + 512), writes=[f"w{i0 + 1}"])
                            issued6[kk] = (wbuf[i0], f"w{i0}", wbuf[i0 + 1], f"w{i0 + 1}")
                    return issued6[k]
                for br in range(2):
                    P.dma("sp", att[:], s_attn[br], reads=["s_attnA" if br == 0 else "s_attnB"], writes=["att"])
                    for cb in range(4):
                        wg, WG, wo, WO = get_w6(4 * br + cb)
                        for dcl in range(4):
                            dc = 4 * cb + dcl
                            for (g0, gn) in TG:
                                bg = getb()
                                for kc in range(KC):
                                    mm(pb[bg][:, 0:gn], wg[:, kc, dcl * 128:(dcl + 1) * 128], xTo[:, kc, g0:g0 + gn], kc == 0, kc == KC - 1, [WG, "xTo"], [PB[bg]])
                                bo = getb()
                                for kc in range(KC):
                                    mm(pb[bo][:, 0:gn], wo[:, kc, dcl * 128:(dcl + 1) * 128], att[:, kc, g0:g0 + gn], kc == 0, kc == KC - 1, [WO, "att"], [PB[bo]])
                                s_, SG = sg[sn_ % 2], f"sg{sn_ % 2}"
                                t_, TM = tmpm[sn_ % 2], f"tmpm{sn_ % 2}"
                                sn_ += 1
                                P.op("act", lambda e, s_=s_, bg=bg, gn=gn: e.activation(out=s_[:, 0:gn], in_=pb[bg][:, 0:gn], func=AF.Sigmoid), [PB[bg]], [SG])
                                if br == 0:
                                    tt("dve", mrg[:, dc, g0:g0 + gn], s_[:, 0:gn], pb[bo][:, 0:gn], ALU.mult, [SG, PB[bo]], ["mrg"])
                                else:
                                    tt("dve", t_[:, 0:gn], s_[:, 0:gn], pb[bo][:, 0:gn], ALU.mult, [SG, PB[bo]], [TM])
                                    tt("pool", mrg[:, dc, g0:g0 + gn], mrg[:, dc, g0:g0 + gn], t_[:, 0:gn], ALU.add, ["mrg", TM], ["mrg"])
            if DEBUG:
                s_dmrg = dscr("s_dmrg", [128, KC, NOWN])
                P.dma("sp", s_dmrg, mrg[:], reads=["mrg"], writes=["s_dmrg"])
                s_dhpre = dscr("s_dhpre", [NOWN, D], F32)
            P.barrier()
            with ExitStack() as s6b:
                hpre = sb(s6b, "hpre", [128, 9, D], F32)
                wbuf = [sb(s6b, f"w{i}", [128, KC, 512], BF16) for i in range(2)]
                xr = [sb(s6b, f"xr{i}", [128, D], F32) for i in range(2)]
                hb = sb(s6b, "hb", [128, D], BF16)
                hTt = [sb(s6b, f"hTt{i}", [128, KC, 128], BF16) for i in range(2)]
                P.dma("sp", lnc[0][:], ln_d[0], writes=["lnc0"])
                P.dma("sp", lnc[1][:], ln_d[1], writes=["lnc1"])
                issued7 = {}

                def get_w7(k):
                    for kk in (k, k + 1):
                        if kk < 4 and kk not in issued7:
                            P.dma("pool", wbuf[kk % 2][:], wview(w_out, 0, D, 512 * kk, 512 * kk + 512), writes=[f"w{kk % 2}"])
                            issued7[kk] = (wbuf[kk % 2], f"w{kk % 2}")
                    return issued7[k]
                for cb in range(4):
                    wo, WO = get_w7(cb)
                    for ti, (t0, nt) in enumerate(OWN_TILES):
                        bank = getb()
                        for dc in range(KC):
                            mm(pb[bank][:nt, :], mrg[:, dc, t0:t0 + nt], wo[:, dc, :], dc == 0, dc == KC - 1, ["mrg", WO], [PB[bank]])
                        copy(evac_eng(), hpre[:nt, ti, 512 * cb:512 * cb + 512], pb[bank][:nt, :], [PB[bank]], [f"hpre{ti}"])
                for ti, (t0, nt) in enumerate(OWN_TILES):
                    x_, XR_ = xr[ti % 2], f"xr{ti % 2}"
                    HP = f"hpre{ti}"
                    P.dma("sp", x_[:nt, :], xo[t0:t0 + nt, :], writes=[XR_])
                    hv = hpre[:nt, ti, :]
                    P.op("dve", lambda e, x_=x_, hv=hv, nt=nt: e.scalar_tensor_tensor(out=hv, in0=x_[:nt, :], scalar=ALPHA, in1=hv, op0=ALU.mult, op1=ALU.add),
                         [XR_, HP], [HP])
                    if DEBUG:
                        P.dma("sp", s_dhpre[t0:t0 + nt, :], hv, reads=[HP], writes=["s_dhpre"])
                    if DEBUG and ti == 0:
                        s_dd = dscr("s_dd", [4, 128, D], F32)
                    layernorm(hv, HP, nt, 100 if (DEBUG and ti == 0) else 0)
                    if DEBUG:
                        if ti == 0:
                            s_dst = dscr("s_dst", [9, 128, 16], F32)
                        P.dma("sp", s_dst[ti], st[:], reads=["st0", "st1", "st2", "st4", "st5", HP], writes=["s_dst"])
                    P.dma("sp", s_h[t0:t0 + nt, :], hv, reads=[HP], writes=["s_h"])
                    copy("act", hb[:nt, :], hv, [HP], ["hb"])
                    h_, HT = hTt[ti % 2], f"hTt{ti % 2}"
                    for g in range(4):
                        tb = getb()
                        for c in range(4):
                            kc = 4 * g + c
                            mm(pb[tb][:, c * 128:c * 128 + nt], hb[:nt, kc * 128:(kc + 1) * 128], identb[:nt, :nt], True, True, ["hb", "identb"], [PB[tb]])
                        copy(evac_eng(), h_[:, 4 * g:4 * g + 4, 0:nt], pb[tb][:].rearrange("p (a b) -> p a b", a=4)[:, :, 0:nt], [PB[tb]], [HT])
                    P.dma("sp", s_hT[:, :, t0:t0 + nt], h_[:, :, 0:nt], reads=[HT], writes=["s_hT"])
        P.barrier()
        with ExitStack() as s7:
            lnc = [sb(s7, f"lnc{i}", [128, D], F32) for i in range(2)]
            junk2 = sb(s7, "junk2", [128, D], F32)
            hT = sb(s7, "hT", [128, KC, 528], BF16)
            uT = sb(s7, "uT", [128, 64, 528], BF16)
            wbuf = [sb(s7, f"w{i}", [128, KC, 512], BF16) for i in range(2)]
            ypre = sb(s7, "ypre", [128, 5, D], F32)
            hres = sb(s7, "hres", [128, D], F32)
            rtmp = [sb(s7, f"rtmp{i}", [128, 512], F32) for i in range(2)]
            P.dma("sp", lnc[0][:], ln_d[2], writes=["lnc0"])
            P.dma("sp", lnc[1][:], ln_d[3], writes=["lnc1"])

            def layernorm2(xv, XR, nt, lnc=lnc, junk2=junk2):
                P.op("dve", lambda e: e.tensor_reduce(out=st[:nt, 4:5], in_=xv, axis=AX.X, op=ALU.add), [XR], ["st4"])
                P.op("dve", lambda e: e.tensor_scalar(out=st[:nt, 5:6], in0=st[:nt, 4:5], scalar1=1.0 / D, scalar2=None, op0=ALU.mult), ["st4"], ["st5"])
                P.op("dve", lambda e: e.tensor_scalar(out=xv, in0=xv, scalar1=st[:nt, 5:6], scalar2=None, op0=ALU.subtract), [XR, "st5"], [XR])
                rms_scale(xv, XR, nt, D, junk2)
                P.op("dve", lambda e: e.scalar_tensor_tensor(out=xv, in0=xv, scalar=st[:nt, 2:3], in1=lnc[0][:nt, :], op0=ALU.mult, op1=ALU.mult),
                     [XR, "st2", "lnc0"], [XR])
                tt("pool", xv, xv, lnc[1][:nt, :], ALU.add, [XR, "lnc1"], [XR])

            wn = 0
            rn = 0
            for (G0, GN, gtiles) in [(0, 512, [(0, 128), (128, 128), (256, 128), (384, 128)]),
                                     (512, 528, [(0, 128), (128, 128), (256, 128), (384, 128), (512, 16)])]:
                P.dma("sp", hT[:, :, 0:GN], s_hT[:, :, G0:G0 + GN], reads=["s_hT"], writes=["hT"])
                cols = [(0, 512)] + ([(512, 16)] if GN > 512 else [])
                issued8 = {}

                def get_w8(k, wn0=wn):
                    for kk in (k, k + 1):
                        if kk < 32 and kk not in issued8:
                            bi_ = (wn0 + kk) % 2
                            if kk < 16:
                                src_ = wview(w_up, 0, D, 512 * kk, 512 * kk + 512)
                            else:
                                cb_, sub_ = (kk - 16) // 4, (kk - 16) % 4
                                src_ = wview(w_down, 2048 * sub_, 2048 * sub_ + 2048, 512 * cb_, 512 * cb_ + 512)
                            P.dma("pool", wbuf[bi_][:], src_, writes=[f"w{bi_}"])
                            issued8[kk] = (wbuf[bi_], f"w{bi_}")
                    return issued8[k]
                for fb in range(16):
                    wu, WU = get_w8(fb)
                    for fl in range(4):
                        fc = 4 * fb + fl
                        for (g0, gn) in cols:
                            bank = getb()
                            for kc in range(KC):
                                mm(pb[bank][:, 0:gn], wu[:, kc, fl * 128:(fl + 1) * 128], hT[:, kc, g0:g0 + gn], kc == 0, kc == KC - 1, [WU, "hT"], [PB[bank]])
                            r_, RT = rtmp[rn % 2], f"rtmp{rn % 2}"
                            rn += 1
                            P.op("act", lambda e, r_=r_, bank=bank, gn=gn: e.activation(out=r_[:, 0:gn], in_=pb[bank][:, 0:gn], func=AF.Relu), [PB[bank]], [RT])
                            tt("dve" if rn % 2 else "pool", uT[:, fc, g0:g0 + gn], r_[:, 0:gn], r_[:, 0:gn], ALU.mult, [RT], ["uT"])
                for cb in range(4):
                    banks = [holdb() for _ in gtiles]
                    for sub in range(4):
                        wd, WD = get_w8(16 + 4 * cb + sub)
                        for gi, (q0, nt) in enumerate(gtiles):
                            for fcl in range(16):
                                mm(pb[banks[gi]][:nt, :], uT[:, 16 * sub + fcl, q0:q0 + nt], wd[:, fcl, :], sub == 0 and fcl == 0, sub == 3 and fcl == 15,
                                   ["uT", WD], [PB[banks[gi]]])
                    for gi, (q0, nt) in enumerate(gtiles):
                        copy(evac_eng(), ypre[:nt, gi, 512 * cb:512 * cb + 512], pb[banks[gi]][:nt, :], [PB[banks[gi]]], [f"ypre{gi}"])
                        relb(banks[gi])
                for gi, (q0, nt) in enumerate(gtiles):
                    t0 = G0 + q0
                    YP = f"ypre{gi}"
                    P.dma("sp", hres[:nt, :], s_h[t0:t0 + nt, :], reads=["s_h"], writes=["hres"])
                    yv = ypre[:nt, gi, :]
                    P.op("dve", lambda e, yv=yv, nt=nt: e.scalar_tensor_tensor(out=yv, in0=hres[:nt, :], scalar=ALPHA, in1=yv, op0=ALU.mult, op1=ALU.add),
                         ["hres", YP], [YP])
                    layernorm2(yv, YP, nt)
                    P.dma("sp", y[t0:t0 + nt, :], yv, reads=[YP])
    P.finish()
    cstack.close()
    return nc, P.stats


def _rope_tab(pos, rot):
    inv = (500000.0 ** (-(np.arange(0, rot, 2, dtype=np.float32) / np.float32(rot)))).astype(np.float32)
    ang = pos.astype(np.float32)[:, None] * inv[None, :]
    return np.cos(ang).astype(np.float32), np.sin(ang).astype(np.float32)


def _consts(j):
    qa, qb = j, 7 - j
    pos_k = np.arange(SEQ)
    c64, s64 = _rope_tab(pos_k, 64)
    c32, s32 = _rope_tab(pos_k, 32)
    tk = np.concatenate([c64, s64, np.tile(c32, (1, 4)), np.tile(s32, (1, 4))], axis=1).reshape(32, 128, 192)
    pos_s = PAST + np.arange(DEC)
    c64s, s64s = _rope_tab(pos_s, 64)
    c32s, s32s = _rope_tab(pos_s, 32)
    tks = np.zeros((1, 128, 192), np.float32)
    tks[0, :DEC] = np.concatenate([c64s, s64s, np.tile(c32s, (1, 4)), np.tile(s32s, (1, 4))], axis=1)
    tabK = np.concatenate([tk, tks], axis=0).astype(np.float32)
    pos_o = np.concatenate([512 * qa + np.arange(512), 512 * qb + np.arange(512), pos_s])
    c64o, s64o = _rope_tab(pos_o, 64)
    c32o, s32o = _rope_tab(pos_o, 32)
    to = np.concatenate([np.tile(c64o, (1, 2)), np.tile(s64o, (1, 2)), np.tile(c32o, (1, 4)), np.tile(s32o, (1, 4))], axis=1)
    tabQ = np.zeros((9 * 128, 256), np.float32)
    tabQ[:NOWN] = to
    tabQ = tabQ.reshape(9, 128, 256)
    qc = np.stack([np.broadcast_to(((512 * q + np.arange(512)) // 64).astype(np.float32), (128, 512)) for q in (qa, qb)])
    kcc = (2 * np.arange(32)[None, :] + (np.arange(128)[:, None] >= 64)).astype(np.float32)
    tcc = np.zeros((128, 8), np.float32)
    for i in range(8):
        q0 = 512 * (qa if i < 4 else qb) + 128 * (i % 4)
        tcc[:, i] = (q0 + np.arange(128)) // 64
    scb = np.broadcast_to((np.arange(SEQ) // 64).astype(np.float32), (128, SEQ))
    return dict(tabK=tabK, tabQ=tabQ, qc=np.ascontiguousarray(qc), kcc=kcc, tcc=tcc, scb=np.ascontiguousarray(scb),
                ident=np.eye(128, dtype=np.float32))


_CACHE = {}


def kernel(x_prompt, x_sample, cache_mla_latent, cache_mla_rope, cache_dsa_k, cache_dsa_v, cache_dsa_idx_k,
           w_in, g_q_norm, g_kv_norm, w_uq, w_ukv, w_o_mla, w_o_dsa, w_out, ln1_g, ln1_b, w_up, w_down, ln2_g, ln2_b):
    f = lambda a: np.ascontiguousarray(np.asarray(a, dtype=np.float32))
    x_prompt, x_sample = f(x_prompt), f(x_sample)
    if "nc" not in _CACHE:
        _CACHE["nc"] = build()
    nc, stats = _CACHE["nc"]
    bc = lambda v, n: np.ascontiguousarray(np.broadcast_to(f(v).reshape(1, n), (128, n)))
    shared = dict(
        w_in=f(w_in)[0], w_uq=f(w_uq)[0].reshape(512, 3072), w_ukv=f(w_ukv)[0], w_o_mla=f(w_o_mla)[0], w_o_dsa=f(w_o_dsa)[0],
        w_out=f(w_out)[0], w_up=f(w_up)[0], w_down=f(w_down)[0], gq_bc=bc(g_q_norm, 512), gkv_bc=bc(g_kv_norm, 512),
        ln_bc=np.stack([bc(ln1_g, D), bc(ln1_b, D), bc(ln2_g, D), bc(ln2_b, D)]))
    in_maps = []
    for c in range(8):
        b, j = c // 4, c % 4
        qa, qb = j, 7 - j
        xo = np.concatenate([x_prompt[b, 512 * qa:512 * qa + 512], x_prompt[b, 512 * qb:512 * qb + 512], x_sample[c]], axis=0)
        m = dict(shared)
        m.update(_consts(j))
        m.update(xk=x_prompt[b], xo=np.ascontiguousarray(xo), c_lat=f(cache_mla_latent)[0, c], c_rope=f(cache_mla_rope)[0, c],
                 c_k=f(cache_dsa_k)[0, c].reshape(PAST, 512), c_v=f(cache_dsa_v)[0, c].reshape(PAST, 512), c_idx=f(cache_dsa_idx_k)[0, c])
        in_maps.append(m)
    if os.environ.get("K_TRACE"):
        res = run_bass_kernel_spmd(nc, in_maps, core_ids=list(range(8)), trace=True)
        print("EXEC_NS", res.exec_time_ns)
    else:
        res = run_bass_kernel_spmd(nc, in_maps, core_ids=list(range(8)))
    R = res.results
    _CACHE["R"] = R
    y_p = np.zeros((2, SEQ, D), np.float32)
    y_s = np.zeros((8, DEC, D), np.float32)
    for c in range(8):
        b, j = c // 4, c % 4
        qa, qb = j, 7 - j
        yy = R[c]["y"]
        y_p[b, 512 * qa:512 * qa + 512] = yy[0:512]
        y_p[b, 512 * qb:512 * qb + 512] = yy[512:1024]
        y_s[c] = yy[1024:1040]
    pl = lambda k, shp: np.stack([R[0][k], R[4][k]]).reshape((1, 2, SEQ) + shp)
    sl = lambda k, shp: np.stack([R[c][k] for c in range(8)]).reshape((1, 8, DEC) + shp)
    return (y_p, y_s, pl("pk_lat", (512,)), pl("pk_rope", (64,)), pl("pk_k", (4, 128)), pl("pk_v", (4, 128)), pl("pk_idx", (128,)),
            sl("sk_lat", (512,)), sl("sk_rope", (64,)), sl("sk_k", (4, 128)), sl("sk_v", (4, 128)), sl("sk_idx", (128,)))
```

```python
import os
from contextlib import ExitStack
import numpy as np
import concourse.bass as bass
import concourse.mybir as mybir
from concourse.bass_utils import run_bass_kernel_spmd

F32 = mybir.dt.float32
BF16 = mybir.dt.bfloat16
AF = mybir.ActivationFunctionType
ALU = mybir.AluOpType
AX = mybir.AxisListType

ENGS = ("pe", "act", "dve", "pool", "sp")
EPOCH = 12000
DMA_RING = 8
DMA_EPOCH = 1500


class Inst:
    __slots__ = ("eng", "fn", "reads", "writes", "dma", "idx", "sig", "dsem")

    def __init__(self, eng, fn, reads, writes, dma):
        self.eng = eng
        self.fn = fn
        self.reads = reads
        self.writes = writes
        self.dma = dma
        self.sig = None
        self.dsem = None


class Prog:
    def __init__(self, nc):
        self.nc = nc
        self.insts = []
        self.barriers = []
        self.stack = ExitStack()

    def op(self, eng, fn, reads=(), writes=()):
        i = Inst(eng, fn, tuple(reads), tuple(writes), False)
        i.idx = len(self.insts)
        self.insts.append(i)
        return i

    def barrier(self):
        self.barriers.append(len(self.insts))

    def dma(self, q, out, in_, reads=(), writes=()):
        def fn(e, out=out, in_=in_):
            return e.dma_start(out=out, in_=in_)
        i = Inst(q, fn, tuple(reads), tuple(writes), True)
        i.idx = len(self.insts)
        self.insts.append(i)
        return i

    def finish(self):
        nc = self.nc
        insts = self.insts
        last_w = {}
        readers = {}
        need = [None] * len(insts)
        for i in insts:
            d = {}
            for r in i.reads:
                w = last_w.get(r)
                if w is not None:
                    d[w] = True
            for r in i.writes:
                w = last_w.get(r)
                if w is not None:
                    d.setdefault(w, False)
                for rd in readers.get(r, ()):
                    if rd != i.idx:
                        d.setdefault(rd, False)
            for r in i.reads:
                readers.setdefault(r, []).append(i.idx)
            for r in i.writes:
                last_w[r] = i.idx
                readers[r] = []
            keep = []
            best = {}
            for didx, raw in d.items():
                dd = insts[didx]
                if dd.dma:
                    keep.append(didx)
                    continue
                if (not i.dma) and dd.eng == i.eng and (i.eng == "pe" or (not raw and i.eng != "pool")):
                    continue
                if didx > best.get(dd.eng, -1):
                    best[dd.eng] = didx
            need[i.idx] = keep + list(best.values())
        for bpos in self.barriers:
            lastc = {}
            lastd = {e: [] for e in ENGS}
            for i in insts[:bpos]:
                if i.dma:
                    lastd[i.eng].append(i.idx)
                else:
                    lastc[i.eng] = i.idx
            bdeps = list(lastc.values())
            for e in ENGS:
                bdeps += lastd[e][-DMA_RING:]
            seen = set()
            for i in insts[bpos:]:
                if i.eng in seen:
                    continue
                seen.add(i.eng)
                cur = set(need[i.idx])
                for d_ in bdeps:
                    if d_ not in cur and not (insts[d_].eng == i.eng and not insts[d_].dma and not i.dma):
                        need[i.idx].append(d_)
                if len(seen) == len(ENGS):
                    break
        for i in insts:
            for didx in need[i.idx]:
                insts[didx].sig = True
        cnt = {e: 0 for e in ENGS}
        dcnt = {e: 0 for e in ENGS}
        for i in insts:
            if i.dma:
                i.dsem = dcnt[i.eng]
                dcnt[i.eng] += 1
            elif i.sig:
                i.sig = cnt[i.eng]
                cnt[i.eng] += 1
            else:
                i.sig = None
        sems = {}
        for e in ENGS:
            n = max(1, (cnt[e] + EPOCH - 1) // EPOCH)
            sems[e] = [self.stack.enter_context(nc.semaphore(f"s_{e}_{k}")) for k in range(n)]
        dsems = {}
        for e in ENGS:
            if dcnt[e]:
                nep = (dcnt[e] + DMA_RING * DMA_EPOCH - 1) // (DMA_RING * DMA_EPOCH)
                dsems[e] = [[self.stack.enter_context(nc.semaphore(f"d_{e}_{k}_{r}")) for r in range(DMA_RING)]
                            for k in range(nep)]

        def dma_sem_val(q, n):
            r = n % DMA_RING
            m = n // DMA_RING
            return dsems[q][m // DMA_EPOCH][r], 16 * ((m % DMA_EPOCH) + 1)

        def sig_sem_val(e, k):
            return sems[e][k // EPOCH], (k % EPOCH) + 1

        per_eng = {e: [i for i in insts if i.eng == e] for e in ENGS}
        self.stats = {e: len(per_eng[e]) for e in ENGS}
        self.stats.update({f"sig_{e}": cnt[e] for e in ENGS})
        self.stats.update({f"dma_{e}": dcnt[e] for e in ENGS})

        def emit(e, eng):
            waited = {}

            def wait(sem, val):
                key = id(sem)
                if waited.get(key, 0) >= val:
                    return
                waited[key] = val
                eng.wait_ge(sem, val)

            for i in per_eng[e]:
                for didx in sorted(need[i.idx]):
                    d = insts[didx]
                    if d.dma:
                        s, v = dma_sem_val(d.eng, d.dsem)
                    else:
                        s, v = sig_sem_val(d.eng, d.sig)
                    wait(s, v)
                if i.dma:
                    if i.dsem >= DMA_RING:
                        s, v = dma_sem_val(e, i.dsem - DMA_RING)
                        wait(s, v)
                    s, v = dma_sem_val(e, i.dsem)
                    i.fn(eng).then_inc(s, 16)
                else:
                    ins = i.fn(eng)
                    if i.sig is not None:
                        s, v = sig_sem_val(e, i.sig)
                        ins.then_inc(s, 1)
            n = dcnt[e]
            for k in range(max(0, n - DMA_RING), n):
                s, v = dma_sem_val(e, k)
                wait(s, v)

        with nc.Block() as block:
            @block.tensor
            def _(eng):
                emit("pe", eng)

            @block.scalar
            def _(eng):
                emit("act", eng)

            @block.vector
            def _(eng):
                emit("dve", eng)

            @block.gpsimd
            def _(eng):
                emit("pool", eng)

            @block.sync
            def _(eng):
                emit("sp", eng)
        self.stack.close()


D = 2048
KC = 16
SEQ = 4096
PAST = 2048
DEC = 16
NOWN = 1040
LKS = PAST + DEC
C_QLAT, C_KVLAT, C_KPE, C_QB, C_KB, C_VB, C_QI, C_KI, C_WI, C_GA, C_GB = (
    0, 512, 1024, 1088, 3136, 3648, 4160, 6208, 6336, 6352, 8400)
EPS = 1e-6
ALPHA = 2.0 ** 0.25
IDX_SCALE = (16 * 128) ** -0.5
MLA_SCALE = 192.0 ** -0.5
DSA_SCALE = 128.0 ** -0.5
TOPK = 256
NBIS = 24
STAGE = int(os.environ.get("K_STAGE", "9"))
DEBUG = bool(int(os.environ.get("K_DEBUG", "0")))

OWN_TILES = [(128 * i, 128) for i in range(8)] + [(1024, 16)]
KT_P = [(128 * i, 128) for i in range(32)]
KT_S = [(128 * i, 128) for i in range(16)] + [(2048, 16)]
SLOTS = [
    dict(name="A", tok0=0, n=512, kv=0, kts=KT_P[:16], masked=lambda kt: True, tiles=[0, 1, 2, 3]),
    dict(name="B", tok0=512, n=512, kv=0, kts=KT_P, masked=lambda kt: kt >= 16, tiles=[4, 5, 6, 7]),
    dict(name="S", tok0=1024, n=16, kv=1, kts=KT_S, masked=lambda kt: False, tiles=[8]),
]


def build():
    nc = bass.Bass("TRN2", target_bir_lowering=False)
    P = Prog(nc)

    def din(name, shape, dt=F32):
        return nc.dram_tensor(name, list(shape), dt, kind="ExternalInput").ap()

    def dout(name, shape):
        return nc.dram_tensor(name, list(shape), F32, kind="ExternalOutput").ap()

    def dscr(name, shape, dt=BF16):
        return nc.dram_tensor(name, list(shape), dt, kind=("ExternalOutput" if DEBUG else "Internal")).ap()

    xk = din("xk", [SEQ, D])
    xo = din("xo", [NOWN, D])
    c_lat = din("c_lat", [PAST, 512])
    c_rope = din("c_rope", [PAST, 64])
    c_k = din("c_k", [PAST, 512])
    c_v = din("c_v", [PAST, 512])
    c_idx = din("c_idx", [PAST, 128])
    w_in = din("w_in", [D, 10448])
    w_uq = din("w_uq", [512, 3072])
    w_ukv = din("w_ukv", [512, 16, 256])
    w_o_mla = din("w_o_mla", [D, D])
    w_o_dsa = din("w_o_dsa", [D, D])
    w_out = din("w_out", [D, D])
    w_up = din("w_up", [D, 8192])
    w_down = din("w_down", [8192, D])
    gq_d = din("gq_bc", [128, 512])
    gkv_d = din("gkv_bc", [128, 512])
    ln_d = din("ln_bc", [4, 128, D])
    tabK = din("tabK", [33, 128, 192])
    tabQ = din("tabQ", [9, 128, 256])
    qc_d = din("qc", [2, 128, 512])
    kcc_d = din("kcc", [128, 32])
    tcc_d = din("tcc", [128, 8])
    scb_d = din("scb", [128, SEQ])
    ident_d = din("ident", [128, 128])

    y = dout("y", [NOWN, D])
    pk = [dout("pk_lat", [SEQ, 512]), dout("pk_rope", [SEQ, 64]), dout("pk_idx", [SEQ, 128]),
          dout("pk_k", [SEQ, 512]), dout("pk_v", [SEQ, 512])]
    sk = [dout("sk_lat", [DEC, 512]), dout("sk_rope", [DEC, 64]), dout("sk_idx", [DEC, 128]),
          dout("sk_k", [DEC, 512]), dout("sk_v", [DEC, 512])]

    LKP = [SEQ, LKS]
    s_ckvT = [dscr(f"s_ckvT{k}", [128, 4, LKP[k]]) for k in range(2)]
    s_kbT = [dscr(f"s_kbT{k}", [128, 4, LKP[k]]) for k in range(2)]
    s_kiT = [dscr(f"s_kiT{k}", [128, LKP[k]]) for k in range(2)]
    s_kpeT = [dscr(f"s_kpeT{k}", [64, LKP[k]]) for k in range(2)]
    s_vb = [dscr(f"s_vb{k}", [128, (LKP[k] + 127) // 128, 512]) for k in range(2)]
    s_qnT = dscr("s_qnT", [128, 16, NOWN])
    s_qpeT = dscr("s_qpeT", [64, 16, NOWN])
    s_qbT = dscr("s_qbT", [128, 16, NOWN])
    s_qiT = dscr("s_qiT", [128, 16, NOWN])
    s_attn = [dscr("s_attnA", [128, 16, NOWN]), dscr("s_attnB", [128, 16, NOWN])]
    s_h = dscr("s_h", [NOWN, D], F32)

    cstack = ExitStack()

    uniq = [0]

    def sb(stack, name, shape, dt):
        uniq[0] += 1
        return stack.enter_context(nc.sbuf_tensor(f"{name}_{uniq[0]}", list(shape), dt))

    def ps(stack, name, shape, dt=F32):
        return stack.enter_context(nc.psum_tensor(name, list(shape), dt))

    identf = sb(cstack, "identf", [128, 128], F32)
    identb = sb(cstack, "identb", [128, 128], BF16)
    st = sb(cstack, "st", [128, 16], F32)
    P.dma("sp", identf[:], ident_d, writes=["identf"])
    P.op("dve", lambda e: e.tensor_copy(out=identb[:], in_=identf[:]), reads=["identf"], writes=["identb"])

    pb = [ps(cstack, f"pb{i}", [128, 512]) for i in range(8)]
    PB = [f"pb{i}" for i in range(8)]

    tog = [0]

    def evac_eng():
        tog[0] ^= 1
        return "act" if tog[0] else "dve"

    def copy(eng, out, in_, reads, writes, scale=None):
        if eng == "act":
            if scale is None:
                P.op("act", lambda e: e.activation(out=out, in_=in_, func=AF.Copy), reads, writes)
            else:
                P.op("act", lambda e: e.activation(out=out, in_=in_, func=AF.Copy, scale=scale), reads, writes)
        else:
            if scale is None:
                P.op(eng, lambda e: e.tensor_copy(out=out, in_=in_), reads, writes)
            else:
                P.op(eng, lambda e: e.tensor_scalar(out=out, in0=in_, scalar1=scale, scalar2=None, op0=ALU.mult), reads, writes)

    def mm(out, lhsT, rhs, start, stop, reads, writes):
        P.op("pe", lambda e: e.matmul(out, lhsT=lhsT, rhs=rhs, start=start, stop=stop), reads, writes)

    def tt(eng, out, a, b, op, reads, writes):
        P.op(eng, lambda e: e.tensor_tensor(out=out, in0=a, in1=b, op=op), reads, writes)

    def rope(eng, xv, half, cosv, sinv, tmps, nt, H, xres, tres):
        tA, tB, tC, tD = [t[:nt, 0:H, 0:half] for t in tmps]
        x1 = xv[:, :, 0:half]
        x2 = xv[:, :, half:2 * half]
        tt(eng, tA, x1, cosv, ALU.mult, [xres, tres], ["rtA"])
        tt(eng, tB, x2, sinv, ALU.mult, [xres, tres], ["rtB"])
        tt(eng, tC, x2, cosv, ALU.mult, [xres, tres], ["rtC"])
        tt(eng, tD, x1, sinv, ALU.mult, [xres, tres], ["rtD"])
        tt(eng, x1, tA, tB, ALU.subtract, ["rtA", "rtB"], [xres])
        tt(eng, x2, tC, tD, ALU.add, ["rtC", "rtD"], [xres])

    def rms_scale(psrc, psname, nt, width, junk):
        P.op("pool", lambda e: e.memset(st[:, 0:1], 0.0), [], ["st0"])
        P.op("act", lambda e: e.activation(out=junk[:nt, 0:width], in_=psrc, func=AF.Square, accum_out=st[:nt, 0:1]),
             [psname, "st0"], ["junk", "st0"])
        P.op("act", lambda e: e.activation(out=st[:nt, 1:2], in_=st[:nt, 0:1], func=AF.Sqrt, scale=1.0 / width, bias=eps_t[:nt, 0:1]),
             ["st0", "eps"], ["st1"])
        P.op("dve", lambda e: e.reciprocal(out=st[:nt, 2:3], in_=st[:nt, 1:2]), ["st1"], ["st2"])

    eps_t = sb(cstack, "eps_t", [128, 1], F32)
    P.op("pool", lambda e: e.memset(eps_t[:], EPS), [], ["eps"])

    wq = [0]

    def load_x_T(stack_bufs, src_rows, nt, xT_dst3, xT_res, b, bankfn=None, preloaded=False):
        xs = stack_bufs[b]
        if not preloaded:
            P.dma("act", xs[:nt, :], src_rows, writes=[f"xs{b}"])
        for g in range(4):
            bank = bankfn(g) if bankfn else g % 2
            for c in range(4):
                kc = g * 4 + c
                mm(pb[bank][:, c * 128:c * 128 + nt], xs[:nt, kc * 128:(kc + 1) * 128], identf[:nt, :nt], True, True,
                   [f"xs{b}", "identf"], [PB[bank]])
            copy(evac_eng(), xT_dst3(g), pb[bank][:].rearrange("p (a b) -> p a b", a=4)[:, :, 0:nt], [PB[bank]], [xT_res])

    with ExitStack() as s1:
        wk = sb(s1, "wk", [128, KC, 1728], BF16)
        xs = [sb(s1, f"xs{i}", [128, D], F32) for i in range(3)]
        xT = [sb(s1, f"xT{i}", [128, KC, 128], BF16) for i in range(3)]
        rows = [sb(s1, f"rows{i}", [128, 1728], F32) for i in range(3)]
        rowsb = [sb(s1, f"rowsb{i}", [128, 1728], BF16) for i in range(3)]
        kT = [sb(s1, f"kT{i}", [128, 10, 512], BF16) for i in range(2)]
        kgrp = {}
        tab = [sb(s1, f"tab{i}", [128, 192], F32) for i in range(3)]
        tmps = [sb(s1, f"rt{i}", [128, 16, 32], F32) for i in range(4)]
        gkv = sb(s1, "gkv", [128, 512], F32)
        junk = sb(s1, "junk", [128, 512], F32)
        P.dma("sp", gkv[:], gkv_d, writes=["gkv"])
        for (c0, c1, d0) in [(C_KVLAT, C_KVLAT + 512, 0), (C_KPE, C_KPE + 64, 512), (C_KI, C_KI + 128, 576),
                             (C_KB, C_KB + 512, 704), (C_VB, C_VB + 512, 1216)]:
            P.dma("pool", wk[:, :, d0:d0 + (c1 - c0)], w_in[:, c0:c1].rearrange("(kc p) n -> p kc n", p=128), writes=["wk"])

        def kside_finish(kv, t, nt, b):
            rb = rowsb[b]
            RB = f"rowsb{b}"
            P.dma("sp", s_vb[kv][:, t, :], rb[:, 1216:1728], reads=[RB], writes=[f"s_vb{kv}"])
            g, q = t // 4, t % 4
            if (kv, g) not in kgrp:
                kgrp[(kv, g)] = len(kgrp) % 2
            gi = kgrp[(kv, g)]
            k_ = kT[gi]
            KT_ = f"kT{gi}"
            c0 = q * 128
            for c in range(4):
                mm(pb[6][:, c * 128:c * 128 + nt], rb[:nt, c * 128:(c + 1) * 128], identb[:nt, :nt], True, True, [RB, "identb"], [PB[6]])
            copy(evac_eng(), k_[:, 0:4, c0:c0 + nt], pb[6][:].rearrange("p (a b) -> p a b", a=4)[:, :, 0:nt], [PB[6]], [KT_])
            for c in range(4):
                mm(pb[7][:, c * 128:c * 128 + nt], rb[:nt, 704 + c * 128:704 + (c + 1) * 128], identb[:nt, :nt], True, True, [RB, "identb"], [PB[7]])
            copy(evac_eng(), k_[:, 4:8, c0:c0 + nt], pb[7][:].rearrange("p (a b) -> p a b", a=4)[:, :, 0:nt], [PB[7]], [KT_])
            mm(pb[6][:, 0:nt], rb[:nt, 576:704], identb[:nt, :nt], True, True, [RB, "identb"], [PB[6]])
            mm(pb[6][0:64, 128:128 + nt], rb[:nt, 512:576], identb[:nt, :nt], True, True, [RB, "identb"], [PB[6]])
            copy(evac_eng(), k_[:, 8, c0:c0 + nt], pb[6][:, 0:nt], [PB[6]], [KT_])
            copy(evac_eng(), k_[0:64, 9, c0:c0 + nt], pb[6][0:64, 128:128 + nt], [PB[6]], [KT_])
            if q == 3 or nt < 128:
                tk = g * 512
                w = c0 + nt
                P.dma("sp", s_ckvT[kv][:, :, tk:tk + w], k_[:, 0:4, 0:w], reads=[KT_], writes=[f"s_ckvT{kv}"])
                P.dma("sp", s_kbT[kv][:, :, tk:tk + w], k_[:, 4:8, 0:w], reads=[KT_], writes=[f"s_kbT{kv}"])
                P.dma("sp", s_kiT[kv][:, tk:tk + w], k_[:, 8, 0:w], reads=[KT_], writes=[f"s_kiT{kv}"])
                P.dma("sp", s_kpeT[kv][0:64, tk:tk + w], k_[0:64, 9, 0:w], reads=[KT_], writes=[f"s_kpeT{kv}"])

        def kside_proj(kv, t, nt, src_rows, tab_idx, outs, orow0, b):
            load_x_T(xs, src_rows, nt, lambda g: xT[b][:, 4 * g:4 * g + 4, 0:nt], f"xT{b}", b, preloaded=True)
            XT = f"xT{b}"
            for kc in range(KC):
                for (bank, c0, c1) in [(2, 0, 512), (3, 512, 704), (4, 704, 1216), (5, 1216, 1728)]:
                    mm(pb[bank][:nt, 0:c1 - c0], xT[b][:, kc, 0:nt], wk[:, kc, c0:c1], kc == 0, kc == KC - 1, [XT, "wk"], [PB[bank]])
            r = rows[b]
            R = f"rows{b}"
            rms_scale(pb[2][:nt, :], PB[2], nt, 512, junk)
            P.op("dve", lambda e: e.scalar_tensor_tensor(out=r[:nt, 0:512], in0=pb[2][:nt, :], scalar=st[:nt, 2:3], in1=gkv[:nt, :],
                                                        op0=ALU.mult, op1=ALU.mult), [PB[2], "st2", "gkv"], [R])
            copy("act", r[:nt, 512:704], pb[3][:nt, 0:192], [PB[3]], [R])
            copy("act", r[:nt, 704:1216], pb[4][:nt, :], [PB[4]], [R])
            copy("dve", r[:nt, 1216:1728], pb[5][:nt, :], [PB[5]], [R])
            tb = tab[b]
            TB = f"tab{b}"
            rope("pool", r[:nt, 512:576].rearrange("p (h d) -> p h d", h=1), 32,
                 tb[:nt, 0:32].rearrange("p (h d) -> p h d", h=1), tb[:nt, 32:64].rearrange("p (h d) -> p h d", h=1), tmps, nt, 1, R, TB)
            rope("pool", r[:nt, 576:704].rearrange("p (h d) -> p h d", h=1), 16,
                 tb[:nt, 64:80].rearrange("p (h d) -> p h d", h=1), tb[:nt, 128:144].rearrange("p (h d) -> p h d", h=1), tmps, nt, 1, R, TB)
            rope("pool", r[:nt, 704:1216].rearrange("p (h d) -> p h d", h=4), 16,
                 tb[:nt, 64:128].rearrange("p (h d) -> p h d", h=4), tb[:nt, 128:192].rearrange("p (h d) -> p h d", h=4), tmps, nt, 4, R, TB)
            for oi, (c0, c1) in enumerate([(0, 512), (512, 576), (576, 704), (704, 1216), (1216, 1728)]):
                P.dma("sp", outs[oi][orow0:orow0 + nt, :], r[:nt, c0:c1], reads=[R])
            copy("act", rowsb[b][:nt, :], r[:nt, :], [R], [f"rowsb{b}"])
            if pending_fin:
                kside_finish(*pending_fin.pop())
            pending_fin.append((kv, t, nt, b))

        n = 0
        pending_fin = []
        ptiles = [(0, t, 128, xk[t * 128:(t + 1) * 128, :], t, pk, t * 128) for t in range(32)]
        ptiles.append((1, 16, DEC, xo[1024:1040, :], 32, sk, 0))

        def issue_loads(i):
            kv_, t_, nt_, src_, tab_idx_, _, _ = ptiles[i]
            b_ = i % 3
            P.dma("sp", xs[b_][:nt_, :], src_, writes=[f"xs{b_}"])
            P.dma("sp", tab[b_][:], tabK[tab_idx_], writes=[f"tab{b_}"])
        issue_loads(0)
        issue_loads(1)
        for i, pt in enumerate(ptiles):
            if i + 2 < len(ptiles):
                issue_loads(i + 2)
            kside_proj(*pt, n % 3)
            n += 1
        kside_finish(*pending_fin.pop())
        n0 = n

        def issue_cache(t):
            b_ = (n0 + t) % 3
            R_ = f"rows{b_}"
            r0 = t * 128
            P.dma("sp", rows[b_][:, 0:512], c_lat[r0:r0 + 128, :], writes=[R_])
            P.dma("sp", rows[b_][:, 512:576], c_rope[r0:r0 + 128, :], writes=[R_])
            P.dma("sp", rows[b_][:, 576:704], c_idx[r0:r0 + 128, :], writes=[R_])
            P.dma("sp", rows[b_][:, 704:1216], c_k[r0:r0 + 128, :], writes=[R_])
            P.dma("sp", rows[b_][:, 1216:1728], c_v[r0:r0 + 128, :], writes=[R_])
        issue_cache(0)
        issue_cache(1)
        for t in range(16):
            if t + 2 < 16:
                issue_cache(t + 2)
            b = (n0 + t) % 3
            copy("act" if t % 2 else "dve", rowsb[b][:, :], rows[b][:, :], [f"rows{b}"], [f"rowsb{b}"])
            kside_finish(1, t, 128, b)
            n += 1

    held = set()
    rr = [0]

    def getb():
        for _ in range(32):
            b = rr[0] % 8
            rr[0] += 1
            if b not in held:
                return b
        raise RuntimeError("no psum bank")

    def holdb():
        b = getb()
        held.add(b)
        return b

    def relb(b):
        held.discard(b)

    widx = sb(cstack, "widx", [128, 9, 16], F32)
    s_xTo = dscr("s_xTo", [128, KC, NOWN])
    s_hT = dscr("s_hT", [128, KC, NOWN])

    def wview(w_ap, r0, r1, c0, c1):
        return w_ap[r0:r1, c0:c1].rearrange("(kc p) n -> p kc n", p=128)

    if STAGE >= 2:
        P.barrier()
        with ExitStack() as s2:
            xTo = sb(s2, "xTo", [128, KC, NOWN], BF16)
            xs = [sb(s2, f"xs{i}", [128, D], F32) for i in range(2)]
            wbuf = [sb(s2, f"w{i}", [128, KC, 512], BF16) for i in range(3)]
            wuq = sb(s2, "wuq", [128, 4, 3072], BF16)
            qlT = sb(s2, "qlT", [128, 4, NOWN], BF16)
            qln = [sb(s2, f"qln{i}", [128, 512], BF16) for i in range(2)]
            qrow = [sb(s2, f"qrow{i}", [128, 512], F32) for i in range(3)]
            qrb = [sb(s2, f"qrb{i}", [128, 512], BF16) for i in range(3)]
            qT4 = [sb(s2, f"qT4{i}", [128, 4, 512], BF16) for i in range(3)]
            qgrp = {}

            def q4_for(key):
                if key not in qgrp:
                    qgrp[key] = len(qgrp) % 3
                return qgrp[key]
            SLOT_OF = {ti_: (0 if ti_ < 4 else (1 if ti_ < 8 else 2)) for ti_ in range(9)}
            SLOT_T0 = [0, 512, 1024]
            tq = sb(s2, "tq", [128, 9, 256], F32)
            tmps = [sb(s2, f"rt{i}", [128, 16, 32], F32) for i in range(4)]
            gq = sb(s2, "gq", [128, 512], F32)
            junk = sb(s2, "junk", [128, 512], F32)
            P.dma("sp", gq[:], gq_d, writes=["gq"])
            P.dma("sp", tq[:], tabQ.rearrange("t p c -> p t c"), writes=["tq"])
            P.dma("pool", wuq[:], w_uq.rearrange("(kc p) n -> p kc n", p=128), writes=["wuq"])
            for ti, (t0, nt) in enumerate(OWN_TILES):
                load_x_T(xs, xo[t0:t0 + nt, :], nt, lambda g, t0=t0, nt=nt: xTo[:, 4 * g:4 * g + 4, t0:t0 + nt], "xTo", ti % 2,
                         bankfn=lambda g: getb())
            P.dma("sp", s_xTo, xTo[:], reads=["xTo"], writes=["s_xTo"])
            wloads = [(C_QLAT, 512)] + [(C_QB + 512 * k_, 512) for k_ in range(4)] + [(C_QI + 512 * k_, 512) for k_ in range(4)] + [(C_WI, 16)]
            issued = {}

            def get_w(k):
                for kk in (k, k + 1):
                    if kk < len(wloads) and kk not in issued:
                        buf, res = wbuf[kk % 3], f"w{kk % 3}"
                        c0_, nc_ = wloads[kk]
                        P.dma("pool", buf[:, :, 0:nc_], wview(w_in, 0, D, c0_, c0_ + nc_), writes=[res])
                        issued[kk] = (buf, res)
                return issued[k]

            wb, WB = get_w(0)
            for ti, (t0, nt) in enumerate(OWN_TILES):
                bank = getb()
                for kc in range(KC):
                    mm(pb[bank][:nt, :], xTo[:, kc, t0:t0 + nt], wb[:, kc, 0:512], kc == 0, kc == KC - 1, ["xTo", WB], [PB[bank]])
                rms_scale(pb[bank][:nt, :], PB[bank], nt, 512, junk)
                q_ = qln[ti % 2]
                QN = f"qln{ti % 2}"
                P.op("dve", lambda e, q_=q_, bank=bank, nt=nt: e.scalar_tensor_tensor(
                    out=q_[:nt, :], in0=pb[bank][:nt, :], scalar=st[:nt, 2:3], in1=gq[:nt, :], op0=ALU.mult, op1=ALU.mult),
                    [PB[bank], "st2", "gq"], [QN])
                tb = getb()
                for c in range(4):
                    mm(pb[tb][:, c * 128:c * 128 + nt], q_[:nt, c * 128:(c + 1) * 128], identb[:nt, :nt], True, True, [QN, "identb"], [PB[tb]])
                copy(evac_eng(), qlT[:, 0:4, t0:t0 + nt], pb[tb][:].rearrange("p (a b) -> p a b", a=4)[:, :, 0:nt], [PB[tb]], ["qlT"])
            n = 0
            pend2 = []
            for blk in range(8):
                for ti, (t0, nt) in enumerate(OWN_TILES):
                    i2 = n % 3
                    n += 1
                    bank = getb()
                    for rc in range(4):
                        mm(pb[bank][:nt, 0:384], qlT[:, rc, t0:t0 + nt], wuq[:, rc, blk * 384:(blk + 1) * 384], rc == 0, rc == 3,
                           ["qlT", "wuq"], [PB[bank]])
                    qr_, QR = qrow[i2], f"qrow{i2}"
                    copy("act", qr_[:nt, 0:384], pb[bank][:nt, 0:384], [PB[bank]], [QR])
                    rope("pool", qr_[:nt, 0:384].rearrange("p (h d) -> p h d", h=2)[:, :, 128:192], 32,
                         tq[:nt, ti, 0:64].rearrange("p (h d) -> p h d", h=2), tq[:nt, ti, 64:128].rearrange("p (h d) -> p h d", h=2),
                         tmps, nt, 2, QR, "tq")
                    qb_, QB = qrb[i2], f"qrb{i2}"
                    copy("dve", qb_[:nt, 0:384], qr_[:nt, 0:384], [QR], [QB])

                    def fin_b(qb_=qb_, QB=QB, nt=nt, t0=t0, blk=blk, ti=ti):
                        tb = getb()
                        for i in range(2):
                            mm(pb[tb][:, i * 128:i * 128 + nt], qb_[:nt, i * 192:i * 192 + 128], identb[:nt, :nt], True, True, [QB, "identb"], [PB[tb]])
                        for i in range(2):
                            mm(pb[tb][0:64, (2 + i) * 128:(2 + i) * 128 + nt], qb_[:nt, i * 192 + 128:(i + 1) * 192], identb[:nt, :nt], True, True,
                               [QB, "identb"], [PB[tb]])
                        sl = SLOT_OF[ti]
                        gq = q4_for(("b", blk, sl))
                        q4, Q4 = qT4[gq], f"qT4{gq}"
                        c0 = t0 - SLOT_T0[sl]
                        pv = pb[tb][:].rearrange("p (a b) -> p a b", a=4)
                        copy(evac_eng(), q4[:, 0:2, c0:c0 + nt], pv[:, 0:2, 0:nt], [PB[tb]], [Q4])
                        copy(evac_eng(), q4[0:64, 2:4, c0:c0 + nt], pv[0:64, 2:4, 0:nt], [PB[tb]], [Q4])
                        if ti in (3, 7, 8):
                            w = c0 + nt
                            s0_ = SLOT_T0[sl]
                            P.dma("sp", s_qnT[:, 2 * blk:2 * blk + 2, s0_:s0_ + w], q4[:, 0:2, 0:w], reads=[Q4], writes=["s_qnT"])
                            P.dma("sp", s_qpeT[0:64, 2 * blk:2 * blk + 2, s0_:s0_ + w], q4[0:64, 2:4, 0:w], reads=[Q4], writes=["s_qpeT"])
                    if pend2:
                        pend2.pop()()
                    pend2.append(fin_b)
            for (kbase, dst, DST) in [(1, s_qbT, "s_qbT"), (5, s_qiT, "s_qiT")]:
                for blk in range(4):
                    wb, WB = get_w(kbase + blk)
                    for ti, (t0, nt) in enumerate(OWN_TILES):
                        i2 = n % 3
                        n += 1
                        bank = getb()
                        for kc in range(KC):
                            mm(pb[bank][:nt, :], xTo[:, kc, t0:t0 + nt], wb[:, kc, 0:512], kc == 0, kc == KC - 1, ["xTo", WB], [PB[bank]])
                        qr_, QR = qrow[i2], f"qrow{i2}"
                        copy("act", qr_[:nt, :], pb[bank][:nt, :], [PB[bank]], [QR])
                        rope("pool", qr_[:nt, :].rearrange("p (h d) -> p h d", h=4), 16,
                             tq[:nt, ti, 128:192].rearrange("p (h d) -> p h d", h=4), tq[:nt, ti, 192:256].rearrange("p (h d) -> p h d", h=4),
                             tmps, nt, 4, QR, "tq")
                        qb_, QB = qrb[i2], f"qrb{i2}"
                        copy("dve", qb_[:nt, :], qr_[:nt, :], [QR], [QB])

                        def fin_c(qb_=qb_, QB=QB, nt=nt, t0=t0, blk=blk, dst=dst, DST=DST, ti=ti):
                            tb = getb()
                            for i in range(4):
                                mm(pb[tb][:, i * 128:i * 128 + nt], qb_[:nt, i * 128:(i + 1) * 128], identb[:nt, :nt], True, True, [QB, "identb"], [PB[tb]])
                            sl = SLOT_OF[ti]
                            gq = q4_for((DST, blk, sl))
                            q4, Q4 = qT4[gq], f"qT4{gq}"
                            c0 = t0 - SLOT_T0[sl]
                            copy(evac_eng(), q4[:, 0:4, c0:c0 + nt], pb[tb][:].rearrange("p (a b) -> p a b", a=4)[:, :, 0:nt], [PB[tb]], [Q4])
                            if ti in (3, 7, 8):
                                w = c0 + nt
                                s0_ = SLOT_T0[sl]
                                P.dma("sp", dst[:, 4 * blk:4 * blk + 4, s0_:s0_ + w], q4[:, 0:4, 0:w], reads=[Q4], writes=[DST])
                        if pend2:
                            pend2.pop()()
                        pend2.append(fin_c)
            if pend2:
                pend2.pop()()
            wb, WB = get_w(9)
            for ti, (t0, nt) in enumerate(OWN_TILES):
                bank = getb()
                for kc in range(KC):
                    mm(pb[bank][:nt, 0:16], xTo[:, kc, t0:t0 + nt], wb[:, kc, 0:16], kc == 0, kc == KC - 1, ["xTo", WB], [PB[bank]])
                copy("act", widx[:nt, ti, :], pb[bank][:nt, 0:16], [PB[bank]], ["widx"], scale=IDX_SCALE)

    def attend(slot, h, smm, scale, maskop, vrhs, vres, out_scr, OUT, bufs, qres):
        n = slot["n"]
        tok0 = slot["tok0"]
        kts = slot["kts"]
        nq = (n + 127) // 128
        accb = [holdb() for _ in range(nq)]
        LA = 2
        pend = []
        nk = len(kts)
        for idx in range(nk + LA):
            if idx < nk:
                kt, (k0, kr) = idx, kts[idx]
                bank = getb()
                smm(bank, kt, k0, kr, n)
                i3 = bufs["n"][0] % 4
                bufs["n"][0] += 1
                pT, PT = bufs["pT"][i3], f"pT{i3}"
                P.op("act", lambda e, pT=pT, bank=bank, kr=kr: e.activation(out=pT[:kr, 0:n], in_=pb[bank][:kr, 0:n], func=AF.Exp, scale=scale),
                     [PB[bank]], [PT])
                maskop(kt, kr, n, pT, PT)
                pend.append((pT, PT))
            j = idx - LA
            if j >= 0:
                kt, (k0, kr) = j, kts[j]
                pT, PT = pend[j]
                for qs in range(nq):
                    q0 = qs * 128
                    qr = min(128, n - q0)
                    mm(pb[accb[qs]][:qr, 0:129], pT[:kr, q0:q0 + qr], vrhs(kt, kr), kt == 0, kt == nk - 1, [PT, vres], [PB[accb[qs]]])
        tb = getb()
        for qs in range(nq):
            q0 = qs * 128
            qr = min(128, n - q0)
            a = accb[qs]
            i2 = bufs["n"][0] % 2
            bufs["n"][0] += 1
            rc, RC = bufs["rc"][i2], f"rc{i2}"
            ob, OB = bufs["ob"][i2], f"ob{i2}"
            P.op("dve", lambda e, rc=rc, a=a, qr=qr: e.reciprocal(out=rc[:qr, 0:1], in_=pb[a][:qr, 128:129]), [PB[a]], [RC])
            P.op("dve", lambda e, ob=ob, rc=rc, a=a, qr=qr: e.tensor_scalar(out=ob[:qr, :], in0=pb[a][:qr, 0:128], scalar1=rc[:qr, 0:1],
                                                                           scalar2=None, op0=ALU.mult), [PB[a], RC], [OB])
            mm(pb[tb][:, q0:q0 + qr], ob[:qr, :], identb[:qr, :qr], True, True, [OB, "identb"], [PB[tb]])
        for a in accb:
            relb(a)
        i2 = bufs["n"][0] % 2
        bufs["n"][0] += 1
        oT, OT = bufs["oT"][i2], f"oT{i2}"
        copy(evac_eng(), oT[:, 0:n], pb[tb][:, 0:n], [PB[tb]], [OT])
        P.dma("sp", out_scr[:, h, tok0:tok0 + n], oT[:, 0:n], reads=[OT], writes=[OUT])

    def attn_bufs(stk):
        return dict(n=[0], pT=[sb(stk, f"pT{i}", [128, 512], BF16) for i in range(4)],
                    rc=[sb(stk, f"rc{i}", [128, 1], F32) for i in range(2)],
                    ob=[sb(stk, f"ob{i}", [128, 128], BF16) for i in range(2)],
                    oT=[sb(stk, f"oT{i}", [128, 512], BF16) for i in range(2)])

    if STAGE >= 3:
        P.barrier()
        with ExitStack() as s3:
            ckvT = sb(s3, "ckvT", [128, 4, SEQ], BF16)
            kpeT = sb(s3, "kpeT", [128, SEQ], BF16)
            wkv = [sb(s3, f"wkv{i}", [128, 4, 256], BF16) for i in range(2)]
            knT = [sb(s3, f"knT{i}", [128, SEQ], BF16) for i in range(2)]
            Vh = [sb(s3, f"Vh{i}", [128, 32, 129], BF16) for i in range(2)]
            qn = [sb(s3, f"qn{i}", [128, 512], BF16) for i in range(2)]
            qpe = [sb(s3, f"qpe{i}", [64, 512], BF16) for i in range(2)]
            qc = [sb(s3, f"qc{i}", [128, 512], F32) for i in range(2)]
            kcc = sb(s3, "kcc", [128, 32], F32)
            bufs = attn_bufs(s3)
            for i in range(2):
                P.op("pool", lambda e, i=i: e.memset(Vh[i][:, :, 128:129], 1.0), [], [f"Vh{i}"])
                P.dma("sp", qc[i][:], qc_d[i], writes=[f"qc{i}"])
            P.dma("sp", kcc[:], kcc_d, writes=["kcc"])
            wukv_v = w_ukv.rearrange("(kc p) h e -> p kc h e", p=128)
            hn = 0
            qunits = [(kv_, h_, si_) for kv_ in range(2) for h_ in range(16) for si_, sl_ in enumerate(SLOTS) if sl_["kv"] == kv_]
            q3done = set()

            def issue_q3(u):
                if u >= len(qunits) or u in q3done:
                    return
                q3done.add(u)
                _, h_, si_ = qunits[u]
                n_, tok0_ = SLOTS[si_]["n"], SLOTS[si_]["tok0"]
                j_ = u % 2
                P.dma("sp", qn[j_][:, 0:n_], s_qnT[:, h_, tok0_:tok0_ + n_], reads=["s_qnT"], writes=[f"qn{j_}"])
                P.dma("sp", qpe[j_][:, 0:n_], s_qpeT[:, h_, tok0_:tok0_ + n_], reads=["s_qpeT"], writes=[f"qpe{j_}"])
            for kv in range(2):
                Lk = LKP[kv]
                kts_all = KT_P if kv == 0 else KT_S
                P.dma("sp", ckvT[:, :, 0:Lk], s_ckvT[kv], reads=[f"s_ckvT{kv}"], writes=["ckvT"])
                P.dma("sp", kpeT[0:64, 0:Lk], s_kpeT[kv], reads=[f"s_kpeT{kv}"], writes=["kpeT"])
                for h in range(16):
                    i2 = hn % 2
                    hn += 1
                    wk_, WK = wkv[i2], f"wkv{i2}"
                    kn_, KN = knT[i2], f"knT{i2}"
                    v_, VH = Vh[i2], f"Vh{i2}"
                    P.dma("pool", wk_[:], wukv_v[:, :, h, :], writes=[WK])
                    for s0 in range(0, Lk, 512):
                        sn = min(512, Lk - s0)
                        bank = getb()
                        for rc in range(4):
                            mm(pb[bank][:, 0:sn], wk_[:, rc, 0:128], ckvT[:, rc, s0:s0 + sn], rc == 0, rc == 3, [WK, "ckvT"], [PB[bank]])
                        copy(evac_eng(), kn_[:, s0:s0 + sn], pb[bank][:, 0:sn], [PB[bank]], [KN])
                    full = [(i, k0) for i, (k0, kr) in enumerate(kts_all) if kr == 128]
                    for g0 in range(0, len(full), 4):
                        grp = full[g0:g0 + 4]
                        bank = getb()
                        for gi, (i, k0) in enumerate(grp):
                            for rc in range(4):
                                mm(pb[bank][:, gi * 128:(gi + 1) * 128], ckvT[:, rc, k0:k0 + 128], wk_[:, rc, 128:256], rc == 0, rc == 3,
                                   [WK, "ckvT"], [PB[bank]])
                        ng = len(grp)
                        copy(evac_eng(), v_[:, grp[0][0]:grp[0][0] + ng, 0:128], pb[bank][:].rearrange("p (a b) -> p a b", a=4)[:, 0:ng, :],
                             [PB[bank]], [VH])
                    for i, (k0, kr) in enumerate(kts_all):
                        if kr < 128:
                            bank = getb()
                            for rc in range(4):
                                mm(pb[bank][:kr, 0:128], ckvT[:, rc, k0:k0 + kr], wk_[:, rc, 128:256], rc == 0, rc == 3, [WK, "ckvT"], [PB[bank]])
                            copy(evac_eng(), v_[:kr, i, 0:128], pb[bank][:kr, 0:128], [PB[bank]], [VH])
                    for si, slot in enumerate(SLOTS):
                        if slot["kv"] != kv:
                            continue
                        n, tok0 = slot["n"], slot["tok0"]
                        u = qunits.index((kv, h, si))
                        issue_q3(u)
                        issue_q3(u + 1)
                        j2 = u % 2
                        q_, QN = qn[j2], f"qn{j2}"
                        p_, QP = qpe[j2], f"qpe{j2}"

                        def smm(bank, kt, k0, kr, n, q_=q_, p_=p_, QN=QN, QP=QP, kn_=kn_, KN=KN):
                            mm(pb[bank][:kr, 0:n], kn_[:, k0:k0 + kr], q_[:, 0:n], True, False, [KN, QN], [PB[bank]])
                            mm(pb[bank][:kr, 0:n], kpeT[0:64, k0:k0 + kr], p_[0:64, 0:n], False, True, ["kpeT", QP], [PB[bank]])

                        def maskop(kt, kr, n, pT, PT, slot=slot, si=si):
                            if slot["masked"](kt):
                                P.op("dve", lambda e: e.scalar_tensor_tensor(out=pT[:kr, 0:n], in0=qc[si][:kr, 0:n], scalar=kcc[:kr, kt:kt + 1],
                                                                             in1=pT[:kr, 0:n], op0=ALU.is_ge, op1=ALU.mult),
                                     [f"qc{si}", "kcc", PT], [PT])

                        attend(slot, h, smm, MLA_SCALE, maskop, lambda kt, kr, v_=v_: v_[:kr, kt, 0:129], VH, s_attn[0], "s_attnA", bufs, QN)
    if STAGE >= 4:
        P.barrier()
        with ExitStack() as s4:
            kiT = sb(s4, "kiT", [128, SEQ], BF16)
            scb = sb(s4, "scb", [128, SEQ], BF16)
            score = [sb(s4, f"score{i}", [128, SEQ], F32) for i in range(2)]
            jmask = [sb(s4, f"jmask{i}", [128, SEQ], BF16) for i in range(2)]
            maskT = sb(s4, "maskT", [128, 32, 512], BF16)
            diag = [sb(s4, f"diag{i}", [128, 16, 128], BF16) for i in range(2)]
            qi = [sb(s4, f"qi{i}", [128, 16, 128], BF16) for i in range(2)]
            relu = [sb(s4, f"relu{i}", [128, 512], BF16) for i in range(4)]
            pen = [sb(s4, f"pen{i}", [128, 512], F32) for i in range(2)]
            mnb = [sb(s4, f"mnb{i}", [128, 8], F32) for i in range(2)]
            bs = [sb(s4, f"bs{i}", [128, 8], F32) for i in range(2)]
            tcc = sb(s4, "tcc", [128, 8], F32)
            pw2 = sb(s4, "pw2", [128, NBIS + 1], F32)
            steps = [sb(s4, f"steps{i}", [128, NBIS + 1], F32) for i in range(2)]
            for k_ in range(NBIS + 1):
                P.op("pool", lambda e, k_=k_: e.memset(pw2[:, k_:k_ + 1], 2.0 ** -(k_ + 1)), [], ["pw2"])
            kbT = sb(s4, "kbT", [128, 4, SEQ], BF16)
            vbs = sb(s4, "vbs", [128, 32, 4, 129], BF16)
            qb = [sb(s4, f"qb{i}", [128, 512], BF16) for i in range(2)]
            bufs = attn_bufs(s4)
            P.dma("pool", scb[:], scb_d, writes=["scb"])
            P.dma("sp", tcc[:], tcc_d, writes=["tcc"])
            P.op("pool", lambda e: e.memset(vbs[:, :, :, 128:129], 1.0), [], ["vbs"])
            rnbox = [0]
            qbn = 0
            for si, slot in enumerate(SLOTS):
                kv = slot["kv"]
                kts = slot["kts"]
                Lk = kts[-1][0] + kts[-1][1]
                n, tok0 = slot["n"], slot["tok0"]
                P.dma("sp", kiT[:, 0:Lk], s_kiT[kv][:, 0:Lk], reads=[f"s_kiT{kv}"], writes=["kiT"])
                blocks = [(s0, min(512, Lk - s0)) for s0 in range(0, Lk, 512)]
                def gen_idx(ti, par, Lk=Lk, blocks=blocks, slot=slot, kts=kts, tok0=tok0):
                    nonlocal_rn = rnbox
                    t0, tr = OWN_TILES[ti]
                    q_, QI = qi[par], f"qi{par}"
                    dg, DG = diag[par], f"diag{par}"
                    sc, SC = score[par], f"score{par}"
                    mn_, MN = mnb[par], f"mnb{par}"
                    P.dma("sp", q_[:, :, 0:tr], s_qiT[:, :, t0:t0 + tr], reads=["s_qiT"], writes=[QI])
                    for h in range(16):
                        P.op("pool", lambda e, h=h: e.tensor_scalar(out=dg[:tr, h, 0:tr], in0=identb[:tr, :tr],
                                                                    scalar1=widx[:tr, ti, h:h + 1], scalar2=None, op0=ALU.mult),
                             ["identb", "widx"], [DG])
                    yield
                    for bi, (s0, sn) in enumerate(blocks):
                        sbank = holdb()
                        LA = 2
                        pend = []
                        for hh in range(16 + LA):
                            if hh < 16:
                                h = hh
                                bank = getb()
                                mm(pb[bank][:tr, 0:sn], q_[:, h, 0:tr], kiT[:, s0:s0 + sn], True, True, [QI, "kiT"], [PB[bank]])
                                r_, RL = relu[nonlocal_rn[0] % 4], f"relu{nonlocal_rn[0] % 4}"
                                nonlocal_rn[0] += 1
                                P.op("act", lambda e, r_=r_, bank=bank, sn=sn: e.activation(out=r_[:tr, 0:sn], in_=pb[bank][:tr, 0:sn], func=AF.Relu),
                                     [PB[bank]], [RL])
                                pend.append((r_, RL))
                            h = hh - LA
                            if h >= 0:
                                r_, RL = pend[h]
                                mm(pb[sbank][:tr, 0:sn], dg[:tr, h, 0:tr], r_[:tr, 0:sn], h == 0, h == 15, [DG, RL], [PB[sbank]])
                            if hh % 4 == 3:
                                yield
                        P.op("dve", lambda e, sbank=sbank, bi=bi, sn=sn: e.tensor_reduce(out=mn_[:tr, bi:bi + 1], in_=pb[sbank][:tr, 0:sn], axis=AX.X, op=ALU.min),
                             [PB[sbank]], [MN])
                        if slot["name"] != "S":
                            p_, PN = pen[bi % 2], f"pen{bi % 2}"
                            P.op("pool", lambda e, p_=p_, s0=s0, sn=sn: e.tensor_scalar(out=p_[:tr, 0:sn], in0=scb[:tr, s0:s0 + sn], scalar1=tcc[:tr, ti:ti + 1],
                                                                                      scalar2=1e30, op0=ALU.is_gt, op1=ALU.mult),
                                 ["scb", "tcc"], [PN])
                            tt("dve", sc[:tr, s0:s0 + sn], pb[sbank][:tr, 0:sn], p_[:tr, 0:sn], ALU.subtract, [PB[sbank], PN], [SC])
                        else:
                            copy("dve", sc[:tr, s0:s0 + sn], pb[sbank][:tr, 0:sn], [PB[sbank]], [SC])
                        relb(sbank)
                        yield

                def gen_bis(ti, par, Lk=Lk, blocks=blocks, slot=slot, kts=kts, tok0=tok0):
                    t0, tr = OWN_TILES[ti]
                    sc, SC = score[par], f"score{par}"
                    mn_, MN = mnb[par], f"mnb{par}"
                    jm, JM = jmask[par], f"jmask{par}"
                    b_ = bs[par]
                    B = lambda nme: f"b_{nme}{par}"
                    nb = len(blocks)
                    st_, STP = steps[par], f"steps{par}"
                    P.op("dve", lambda e: e.tensor_reduce(out=b_[:tr, 5:6], in_=sc[:tr, 0:Lk], axis=AX.X, op=ALU.max), [SC], [B("mx")])
                    P.op("dve", lambda e: e.tensor_reduce(out=b_[:tr, 0:1], in_=mn_[:tr, 0:nb], axis=AX.X, op=ALU.min), [MN], [B("lo")])
                    tt("dve", b_[:tr, 1:2], b_[:tr, 5:6], b_[:tr, 0:1], ALU.subtract, [B("mx"), B("lo")], [B("rng")])
                    P.op("dve", lambda e: e.tensor_scalar(out=st_[:tr, :], in0=pw2[:tr, :], scalar1=b_[:tr, 1:2], scalar2=None, op0=ALU.mult),
                         ["pw2", B("rng")], [STP])
                    tt("dve", b_[:tr, 2:3], b_[:tr, 0:1], st_[:tr, 0:1], ALU.add, [B("lo"), STP], [B("cand")])
                    yield
                    for it in range(NBIS):
                        P.op("dve", lambda e: e.tensor_scalar(out=jm[:tr, 0:Lk], in0=sc[:tr, 0:Lk], scalar1=b_[:tr, 2:3], scalar2=0.0,
                                                            op0=ALU.is_ge, op1=ALU.add, accum_out=b_[:tr, 3:4]),
                             [SC, B("cand")], [JM, B("cnt")])
                        P.op("dve", lambda e: e.tensor_scalar(out=b_[:tr, 4:5], in0=b_[:tr, 3:4], scalar1=TOPK - 0.5, scalar2=0.5,
                                                            op0=ALU.is_ge, op1=ALU.subtract), [B("cnt")], [B("inc")])
                        P.op("dve", lambda e, it=it: e.scalar_tensor_tensor(out=b_[:tr, 2:3], in0=b_[:tr, 4:5], scalar=st_[:tr, it:it + 1], in1=b_[:tr, 2:3],
                                                                           op0=ALU.mult, op1=ALU.add), [B("inc"), STP, B("cand")], [B("cand")])
                        yield
                    tt("dve", b_[:tr, 0:1], b_[:tr, 2:3], st_[:tr, NBIS:NBIS + 1], ALU.subtract, [B("cand"), STP], [B("lo")])
                    P.op("dve", lambda e: e.tensor_scalar(out=jm[:tr, 0:Lk], in0=sc[:tr, 0:Lk], scalar1=b_[:tr, 0:1], scalar2=None, op0=ALU.is_ge),
                         [SC, B("lo")], [JM])
                    tq0 = t0 - tok0
                    full = [(i, k0) for i, (k0, kr) in enumerate(kts) if kr == 128]
                    for g0 in range(0, len(full), 4):
                        grp = full[g0:g0 + 4]
                        bank = getb()
                        for gi, (i, k0) in enumerate(grp):
                            mm(pb[bank][:, gi * 128:gi * 128 + tr], jm[:tr, k0:k0 + 128], identb[:tr, :tr], True, True, [JM, "identb"], [PB[bank]])
                        ng = len(grp)
                        copy("act", maskT[:, grp[0][0]:grp[0][0] + ng, tq0:tq0 + tr],
                             pb[bank][:].rearrange("p (a b) -> p a b", a=4)[:, 0:ng, 0:tr], [PB[bank]], ["maskT"])
                        yield
                    for i, (k0, kr) in enumerate(kts):
                        if kr < 128:
                            bank = getb()
                            mm(pb[bank][:kr, 0:tr], jm[:tr, k0:k0 + kr], identb[:tr, :tr], True, True, [JM, "identb"], [PB[bank]])
                            copy("act", maskT[:kr, i, tq0:tq0 + tr], pb[bank][:kr, 0:tr], [PB[bank]], ["maskT"])

                tl = slot["tiles"]
                for _ in gen_idx(tl[0], 0):
                    pass
                for k in range(len(tl)):
                    gb = gen_bis(tl[k], k % 2)
                    gi_ = gen_idx(tl[k + 1], (k + 1) % 2) if k + 1 < len(tl) else iter(())
                    bdone = idone = False
                    while not (bdone and idone):
                        if not bdone:
                            try:
                                next(gb)
                            except StopIteration:
                                bdone = True
                        for _ in range(2):
                            if not idone:
                                try:
                                    next(gi_)
                                except StopIteration:
                                    idone = True
                nkt = len(kts)
                P.dma("sp", kbT[:, :, 0:Lk], s_kbT[kv][:, :, 0:Lk], reads=[f"s_kbT{kv}"], writes=["kbT"])
                P.dma("sp", vbs[:, 0:nkt, :, 0:128], s_vb[kv][:, 0:nkt, :].rearrange("p t (g d) -> p t g d", g=4), reads=[f"s_vb{kv}"], writes=["vbs"])
                mtog = [0]
                q5done = set()

                def issue_q5(h_, n=n, tok0=tok0, base=qbn):
                    if h_ >= 16 or h_ in q5done:
                        return
                    q5done.add(h_)
                    j_ = (base + h_) % 2
                    P.dma("sp", qb[j_][:, 0:n], s_qbT[:, h_, tok0:tok0 + n], reads=["s_qbT"], writes=[f"qb{j_}"])
                for h in range(16):
                    g = h // 4
                    issue_q5(h)
                    issue_q5(h + 1)
                    q_, QB = qb[qbn % 2], f"qb{qbn % 2}"
                    qbn += 1

                    def smm(bank, kt, k0, kr, n, q_=q_, QB=QB, g=g):
                        mm(pb[bank][:kr, 0:n], kbT[:, g, k0:k0 + kr], q_[:, 0:n], True, True, ["kbT", QB], [PB[bank]])

                    def maskop(kt, kr, n, pT, PT):
                        mtog[0] ^= 1
                        tt("dve", pT[:kr, 0:n], pT[:kr, 0:n], maskT[:kr, kt, 0:n], ALU.mult, [PT, "maskT"], [PT])

                    attend(slot, h, smm, DSA_SCALE, maskop, lambda kt, kr, g=g: vbs[:kr, kt, g, 0:129], "vbs", s_attn[1], "s_attnB", bufs, QB)

    if STAGE >= 5:
        TG = [(0, 512), (512, 512), (1024, 16)]
        P.barrier()
        with ExitStack() as s6:
            mrg = sb(s6, "mrg", [128, KC, NOWN], BF16)
            lnc = [sb(s6, f"lnc{i}", [128, D], F32) for i in range(2)]
            junk2 = sb(s6, "junk2", [128, D], F32)

            def layernorm(xv, XR, nt, gi, lnc=lnc, junk2=junk2):
                P.op("dve", lambda e: e.tensor_reduce(out=st[:nt, 4:5], in_=xv, axis=AX.X, op=ALU.add), [XR], ["st4"])
                P.op("dve", lambda e: e.tensor_scalar(out=st[:nt, 5:6], in0=st[:nt, 4:5], scalar1=1.0 / D, scalar2=None, op0=ALU.mult), ["st4"], ["st5"])
                P.op("dve", lambda e: e.tensor_scalar(out=xv, in0=xv, scalar1=st[:nt, 5:6], scalar2=None, op0=ALU.subtract), [XR, "st5"], [XR])
                if DEBUG and gi == 100:
                    P.dma("sp", s_dd[0], xv, reads=[XR], writes=["s_dd"])
                rms_scale(xv, XR, nt, D, junk2)
                P.op("dve", lambda e: e.scalar_tensor_tensor(out=xv, in0=xv, scalar=st[:nt, 2:3], in1=lnc[0][:nt, :], op0=ALU.mult, op1=ALU.mult),
                     [XR, "st2", "lnc0"], [XR])
                if DEBUG and gi == 100:
                    P.dma("sp", s_dd[1], xv, reads=[XR], writes=["s_dd"])
                    P.dma("sp", s_dd[2], lnc[0][:], reads=["lnc0"], writes=["s_dd"])
                    P.dma("sp", s_dd[3], lnc[1][:], reads=["lnc1"], writes=["s_dd"])
                tt("pool", xv, xv, lnc[1][:nt, :], ALU.add, [XR, "lnc1"], [XR])

            with ExitStack() as s6a:
                xTo = sb(s6a, "xTo", [128, KC, NOWN], BF16)
                att = sb(s6a, "att", [128, KC, NOWN], BF16)
                wbuf = [sb(s6a, f"w{i}", [128, KC, 512], BF16) for i in range(4)]
                sg = [sb(s6a, f"sg{i}", [128, 512], F32) for i in range(2)]
                tmpm = [sb(s6a, f"tmpm{i}", [128, 512], F32) for i in range(2)]
                P.dma("sp", xTo[:], s_xTo, reads=["s_xTo"], writes=["xTo"])
                sn_ = 0
                issued6 = {}

                def get_w6(k):
                    for kk in (k, k + 1):
                        if kk < 8 and kk not in issued6:
                            br_, cb_ = kk // 4, kk % 4
                            cg_ = C_GA if br_ == 0 else C_GB
                            wo_ap_ = w_o_mla if br_ == 0 else w_o_dsa
                            i0 = (2 * kk) % 4
                            P.dma("pool", wbuf[i0][:], wview(w_in, 0, D, cg_ + 512 * cb_, cg_ + 512 * cb_ + 512), writes=[f"w{i0}"])
                            P.dma("pool", wbuf[i0 + 1][:], wview(wo_ap_, 0, D, 512 * cb_, 512 * cb_ + 512), writes=[f"w{i0 + 1}"])
                            issued6[kk] = (wbuf[i0], f"w{i0}", wbuf[i0 + 1], f"w{i0 + 1}")
                    return issued6[k]
                for br in range(2):
                    P.dma("sp", att[:], s_attn[br], reads=["s_attnA" if br == 0 else "s_attnB"], writes=["att"])
                    for cb in range(4):
                        wg, WG, wo, WO = get_w6(4 * br + cb)
                        for dcl in range(4):
                            dc = 4 * cb + dcl
                            for (g0, gn) in TG:
                                bg = getb()
                                for kc in range(KC):
                                    mm(pb[bg][:, 0:gn], wg[:, kc, dcl * 128:(dcl + 1) * 128], xTo[:, kc, g0:g0 + gn], kc == 0, kc == KC - 1, [WG, "xTo"], [PB[bg]])
                                bo = getb()
                                for kc in range(KC):
                                    mm(pb[bo][:, 0:gn], wo[:, kc, dcl * 128:(dcl + 1) * 128], att[:, kc, g0:g0 + gn], kc == 0, kc == KC - 1, [WO, "att"], [PB[bo]])
                                s_, SG = sg[sn_ % 2], f"sg{sn_ % 2}"
                                t_, TM = tmpm[sn_ % 2], f"tmpm{sn_ % 2}"
                                sn_ += 1
                                P.op("act", lambda e, s_=s_, bg=bg, gn=gn: e.activation(out=s_[:, 0:gn], in_=pb[bg][:, 0:gn], func=AF.Sigmoid), [PB[bg]], [SG])
                                if br == 0:
                                    tt("dve", mrg[:, dc, g0:g0 + gn], s_[:, 0:gn], pb[bo][:, 0:gn], ALU.mult, [SG, PB[bo]], ["mrg"])
                                else:
                                    tt("dve", t_[:, 0:gn], s_[:, 0:gn], pb[bo][:, 0:gn], ALU.mult, [SG, PB[bo]], [TM])
                                    tt("pool", mrg[:, dc, g0:g0 + gn], mrg[:, dc, g0:g0 + gn], t_[:, 0:gn], ALU.add, ["mrg", TM], ["mrg"])
            if DEBUG:
                s_dmrg = dscr("s_dmrg", [128, KC, NOWN])
                P.dma("sp", s_dmrg, mrg[:], reads=["mrg"], writes=["s_dmrg"])
                s_dhpre = dscr("s_dhpre", [NOWN, D], F32)
            P.barrier()
            with ExitStack() as s6b:
                hpre = sb(s6b, "hpre", [128, 9, D], F32)
                wbuf = [sb(s6b, f"w{i}", [128, KC, 512], BF16) for i in range(2)]
                xr = [sb(s6b, f"xr{i}", [128, D], F32) for i in range(2)]
                hb = sb(s6b, "hb", [128, D], BF16)
                hTt = [sb(s6b, f"hTt{i}", [128, KC, 128], BF16) for i in range(2)]
                P.dma("sp", lnc[0][:], ln_d[0], writes=["lnc0"])
                P.dma("sp", lnc[1][:], ln_d[1], writes=["lnc1"])
                issued7 = {}

                def get_w7(k):
                    for kk in (k, k + 1):
                        if kk < 4 and kk not in issued7:
                            P.dma("pool", wbuf[kk % 2][:], wview(w_out, 0, D, 512 * kk, 512 * kk + 512), writes=[f"w{kk % 2}"])
                            issued7[kk] = (wbuf[kk % 2], f"w{kk % 2}")
                    return issued7[k]
                for cb in range(4):
                    wo, WO = get_w7(cb)
                    for ti, (t0, nt) in enumerate(OWN_TILES):
                        bank = getb()
                        for dc in range(KC):
                            mm(pb[bank][:nt, :], mrg[:, dc, t0:t0 + nt], wo[:, dc, :], dc == 0, dc == KC - 1, ["mrg", WO], [PB[bank]])
                        copy(evac_eng(), hpre[:nt, ti, 512 * cb:512 * cb + 512], pb[bank][:nt, :], [PB[bank]], [f"hpre{ti}"])
                for ti, (t0, nt) in enumerate(OWN_TILES):
                    x_, XR_ = xr[ti % 2], f"xr{ti % 2}"
                    HP = f"hpre{ti}"
                    P.dma("sp", x_[:nt, :], xo[t0:t0 + nt, :], writes=[XR_])
                    hv = hpre[:nt, ti, :]
                    P.op("dve", lambda e, x_=x_, hv=hv, nt=nt: e.scalar_tensor_tensor(out=hv, in0=x_[:nt, :], scalar=ALPHA, in1=hv, op0=ALU.mult, op1=ALU.add),
                         [XR_, HP], [HP])
                    if DEBUG:
                        P.dma("sp", s_dhpre[t0:t0 + nt, :], hv, reads=[HP], writes=["s_dhpre"])
                    if DEBUG and ti == 0:
                        s_dd = dscr("s_dd", [4, 128, D], F32)
                    layernorm(hv, HP, nt, 100 if (DEBUG and ti == 0) else 0)
                    if DEBUG:
                        if ti == 0:
                            s_dst = dscr("s_dst", [9, 128, 16], F32)
                        P.dma("sp", s_dst[ti], st[:], reads=["st0", "st1", "st2", "st4", "st5", HP], writes=["s_dst"])
                    P.dma("sp", s_h[t0:t0 + nt, :], hv, reads=[HP], writes=["s_h"])
                    copy("act", hb[:nt, :], hv, [HP], ["hb"])
                    h_, HT = hTt[ti % 2], f"hTt{ti % 2}"
                    for g in range(4):
                        tb = getb()
                        for c in range(4):
                            kc = 4 * g + c
                            mm(pb[tb][:, c * 128:c * 128 + nt], hb[:nt, kc * 128:(kc + 1) * 128], identb[:nt, :nt], True, True, ["hb", "identb"], [PB[tb]])
                        copy(evac_eng(), h_[:, 4 * g:4 * g + 4, 0:nt], pb[tb][:].rearrange("p (a b) -> p a b", a=4)[:, :, 0:nt], [PB[tb]], [HT])
                    P.dma("sp", s_hT[:, :, t0:t0 + nt], h_[:, :, 0:nt], reads=[HT], writes=["s_hT"])
        P.barrier()
        with ExitStack() as s7:
            lnc = [sb(s7, f"lnc{i}", [128, D], F32) for i in range(2)]
            junk2 = sb(s7, "junk2", [128, D], F32)
            hT = sb(s7, "hT", [128, KC, 528], BF16)
            uT = sb(s7, "uT", [128, 64, 528], BF16)
            wbuf = [sb(s7, f"w{i}", [128, KC, 512], BF16) for i in range(2)]
            ypre = sb(s7, "ypre", [128, 5, D], F32)
            hres = sb(s7, "hres", [128, D], F32)
            rtmp = [sb(s7, f"rtmp{i}", [128, 512], F32) for i in range(2)]
            P.dma("sp", lnc[0][:], ln_d[2], writes=["lnc0"])
            P.dma("sp", lnc[1][:], ln_d[3], writes=["lnc1"])

            def layernorm2(xv, XR, nt, lnc=lnc, junk2=junk2):
                P.op("dve", lambda e: e.tensor_reduce(out=st[:nt, 4:5], in_=xv, axis=AX.X, op=ALU.add), [XR], ["st4"])
                P.op("dve", lambda e: e.tensor_scalar(out=st[:nt, 5:6], in0=st[:nt, 4:5], scalar1=1.0 / D, scalar2=None, op0=ALU.mult), ["st4"], ["st5"])
                P.op("dve", lambda e: e.tensor_scalar(out=xv, in0=xv, scalar1=st[:nt, 5:6], scalar2=None, op0=ALU.subtract), [XR, "st5"], [XR])
                rms_scale(xv, XR, nt, D, junk2)
                P.op("dve", lambda e: e.scalar_tensor_tensor(out=xv, in0=xv, scalar=st[:nt, 2:3], in1=lnc[0][:nt, :], op0=ALU.mult, op1=ALU.mult),
                     [XR, "st2", "lnc0"], [XR])
                tt("pool", xv, xv, lnc[1][:nt, :], ALU.add, [XR, "lnc1"], [XR])

            wn = 0
            rn = 0
            for (G0, GN, gtiles) in [(0, 512, [(0, 128), (128, 128), (256, 128), (384, 128)]),
                                     (512, 528, [(0, 128), (128, 128), (256, 128), (384, 128), (512, 16)])]:
                P.dma("sp", hT[:, :, 0:GN], s_hT[:, :, G0:G0 + GN], reads=["s_hT"], writes=["hT"])
                cols = [(0, 512)] + ([(512, 16)] if GN > 512 else [])
                issued8 = {}

                def get_w8(k, wn0=wn):
                    for kk in (k, k + 1):
                        if kk < 32 and kk not in issued8:
                            bi_ = (wn0 + kk) % 2
                            if kk < 16:
                                src_ = wview(w_up, 0, D, 512 * kk, 512 * kk + 512)
                            else:
                                cb_, sub_ = (kk - 16) // 4, (kk - 16) % 4
                                src_ = wview(w_down, 2048 * sub_, 2048 * sub_ + 2048, 512 * cb_, 512 * cb_ + 512)
                            P.dma("pool", wbuf[bi_][:], src_, writes=[f"w{bi_}"])
                            issued8[kk] = (wbuf[bi_], f"w{bi_}")
                    return issued8[k]
                for fb in range(16):
                    wu, WU = get_w8(fb)
                    for fl in range(4):
                        fc = 4 * fb + fl
                        for (g0, gn) in cols:
                            bank = getb()
                            for kc in range(KC):
                                mm(pb[bank][:, 0:gn], wu[:, kc, fl * 128:(fl + 1) * 128], hT[:, kc, g0:g0 + gn], kc == 0, kc == KC - 1, [WU, "hT"], [PB[bank]])
                            r_, RT = rtmp[rn % 2], f"rtmp{rn % 2}"
                            rn += 1
                            P.op("act", lambda e, r_=r_, bank=bank, gn=gn: e.activation(out=r_[:, 0:gn], in_=pb[bank][:, 0:gn], func=AF.Relu), [PB[bank]], [RT])
                            tt("dve" if rn % 2 else "pool", uT[:, fc, g0:g0 + gn], r_[:, 0:gn], r_[:, 0:gn], ALU.mult, [RT], ["uT"])
                for cb in range(4):
                    banks = [holdb() for _ in gtiles]
                    for sub in range(4):
                        wd, WD = get_w8(16 + 4 * cb + sub)
                        for gi, (q0, nt) in enumerate(gtiles):
                            for fcl in range(16):
                                mm(pb[banks[gi]][:nt, :], uT[:, 16 * sub + fcl, q0:q0 + nt], wd[:, fcl, :], sub == 0 and fcl == 0, sub == 3 and fcl == 15,
                                   ["uT", WD], [PB[banks[gi]]])
                    for gi, (q0, nt) in enumerate(gtiles):
                        copy(evac_eng(), ypre[:nt, gi, 512 * cb:512 * cb + 512], pb[banks[gi]][:nt, :], [PB[banks[gi]]], [f"ypre{gi}"])
                        relb(banks[gi])
                for gi, (q0, nt) in enumerate(gtiles):
                    t0 = G0 + q0
                    YP = f"ypre{gi}"
                    P.dma("sp", hres[:nt, :], s_h[t0:t0 + nt, :], reads=["s_h"], writes=["hres"])
                    yv = ypre[:nt, gi, :]
                    P.op("dve", lambda e, yv=yv, nt=nt: e.scalar_tensor_tensor(out=yv, in0=hres[:nt, :], scalar=ALPHA, in1=yv, op0=ALU.mult, op1=ALU.add),
                         ["hres", YP], [YP])
                    layernorm2(yv, YP, nt)
                    P.dma("sp", y[t0:t0 + nt, :], yv, reads=[YP])
    P.finish()
    cstack.close()
    return nc, P.stats


def _rope_tab(pos, rot):
    inv = (500000.0 ** (-(np.arange(0, rot, 2, dtype=np.float32) / np.float32(rot)))).astype(np.float32)
    ang = pos.astype(np.float32)[:, None] * inv[None, :]
    return np.cos(ang).astype(np.float32), np.sin(ang).astype(np.float32)


def _consts(j):
    qa, qb = j, 7 - j
    pos_k = np.arange(SEQ)
    c64, s64 = _rope_tab(pos_k, 64)
    c32, s32 = _rope_tab(pos_k, 32)
    tk = np.concatenate([c64, s64, np.tile(c32, (1, 4)), np.tile(s32, (1, 4))], axis=1).reshape(32, 128, 192)
    pos_s = PAST + np.arange(DEC)
    c64s, s64s = _rope_tab(pos_s, 64)
    c32s, s32s = _rope_tab(pos_s, 32)
    tks = np.zeros((1, 128, 192), np.float32)
    tks[0, :DEC] = np.concatenate([c64s, s64s, np.tile(c32s, (1, 4)), np.tile(s32s, (1, 4))], axis=1)
    tabK = np.concatenate([tk, tks], axis=0).astype(np.float32)
    pos_o = np.concatenate([512 * qa + np.arange(512), 512 * qb + np.arange(512), pos_s])
    c64o, s64o = _rope_tab(pos_o, 64)
    c32o, s32o = _rope_tab(pos_o, 32)
    to = np.concatenate([np.tile(c64o, (1, 2)), np.tile(s64o, (1, 2)), np.tile(c32o, (1, 4)), np.tile(s32o, (1, 4))], axis=1)
    tabQ = np.zeros((9 * 128, 256), np.float32)
    tabQ[:NOWN] = to
    tabQ = tabQ.reshape(9, 128, 256)
    qc = np.stack([np.broadcast_to(((512 * q + np.arange(512)) // 64).astype(np.float32), (128, 512)) for q in (qa, qb)])
    kcc = (2 * np.arange(32)[None, :] + (np.arange(128)[:, None] >= 64)).astype(np.float32)
    tcc = np.zeros((128, 8), np.float32)
    for i in range(8):
        q0 = 512 * (qa if i < 4 else qb) + 128 * (i % 4)
        tcc[:, i] = (q0 + np.arange(128)) // 64
    scb = np.broadcast_to((np.arange(SEQ) // 64).astype(np.float32), (128, SEQ))
    return dict(tabK=tabK, tabQ=tabQ, qc=np.ascontiguousarray(qc), kcc=kcc, tcc=tcc, scb=np.ascontiguousarray(scb),
                ident=np.eye(128, dtype=np.float32))


_CACHE = {}


def kernel(x_prompt, x_sample, cache_mla_latent, cache_mla_rope, cache_dsa_k, cache_dsa_v, cache_dsa_idx_k,
           w_in, g_q_norm, g_kv_norm, w_uq, w_ukv, w_o_mla, w_o_dsa, w_out, ln1_g, ln1_b, w_up, w_down, ln2_g, ln2_b):
    f = lambda a: np.ascontiguousarray(np.asarray(a, dtype=np.float32))
    x_prompt, x_sample = f(x_prompt), f(x_sample)
    if "nc" not in _CACHE:
        _CACHE["nc"] = build()
    nc, stats = _CACHE["nc"]
    bc = lambda v, n: np.ascontiguousarray(np.broadcast_to(f(v).reshape(1, n), (128, n)))
    shared = dict(
        w_in=f(w_in)[0], w_uq=f(w_uq)[0].reshape(512, 3072), w_ukv=f(w_ukv)[0], w_o_mla=f(w_o_mla)[0], w_o_dsa=f(w_o_dsa)[0],
        w_out=f(w_out)[0], w_up=f(w_up)[0], w_down=f(w_down)[0], gq_bc=bc(g_q_norm, 512), gkv_bc=bc(g_kv_norm, 512),
        ln_bc=np.stack([bc(ln1_g, D), bc(ln1_b, D), bc(ln2_g, D), bc(ln2_b, D)]))
    in_maps = []
    for c in range(8):
        b, j = c // 4, c % 4
        qa, qb = j, 7 - j
        xo = np.concatenate([x_prompt[b, 512 * qa:512 * qa + 512], x_prompt[b, 512 * qb:512 * qb + 512], x_sample[c]], axis=0)
        m = dict(shared)
        m.update(_consts(j))
        m.update(xk=x_prompt[b], xo=np.ascontiguousarray(xo), c_lat=f(cache_mla_latent)[0, c], c_rope=f(cache_mla_rope)[0, c],
                 c_k=f(cache_dsa_k)[0, c].reshape(PAST, 512), c_v=f(cache_dsa_v)[0, c].reshape(PAST, 512), c_idx=f(cache_dsa_idx_k)[0, c])
        in_maps.append(m)
    if os.environ.get("K_TRACE"):
        res = run_bass_kernel_spmd(nc, in_maps, core_ids=list(range(8)), trace=True)
        print("EXEC_NS", res.exec_time_ns)
    else:
        res = run_bass_kernel_spmd(nc, in_maps, core_ids=list(range(8)))
    R = res.results
    _CACHE["R"] = R
    y_p = np.zeros((2, SEQ, D), np.float32)
    y_s = np.zeros((8, DEC, D), np.float32)
    for c in range(8):
        b, j = c // 4, c % 4
        qa, qb = j, 7 - j
        yy = R[c]["y"]
        y_p[b, 512 * qa:512 * qa + 512] = yy[0:512]
        y_p[b, 512 * qb:512 * qb + 512] = yy[512:1024]
        y_s[c] = yy[1024:1040]
    pl = lambda k, shp: np.stack([R[0][k], R[4][k]]).reshape((1, 2, SEQ) + shp)
    sl = lambda k, shp: np.stack([R[c][k] for c in range(8)]).reshape((1, 8, DEC) + shp)
    return (y_p, y_s, pl("pk_lat", (512,)), pl("pk_rope", (64,)), pl("pk_k", (4, 128)), pl("pk_v", (4, 128)), pl("pk_idx", (128,)),
            sl("sk_lat", (512,)), sl("sk_rope", (64,)), sl("sk_k", (4, 128)), sl("sk_v", (4, 128)), sl("sk_idx", (128,)))
```
